# Optimizing a Trainium2 kernel written in Bass

```python
import jax, jax.numpy as jnp
from jax import lax
import numpy as np

D_MODEL = 2048
BATCH = 4
SEQ = 8192
DEPTH = 2

HEAD_DIM = 128
FOX_HEADS = 8
CONV_GROUPS = 8
RET_HEADS = 8
FOX_W = FOX_HEADS * HEAD_DIM
CONV_W = CONV_GROUPS * HEAD_DIM
RET_W = RET_HEADS * HEAD_DIM
N_BRANCH = 3
CONV_K = 3
BLOCK = 128
MEM_LEN = 256
CROSS_HEADS = 4
CROSS_W = CROSS_HEADS * HEAD_DIM
D_FF = ((8 * D_MODEL // 3 + 255) // 256) * 256
ROPE_BASE = 10000.0
EPS = 1e-6
N_IN = 3 * FOX_W + FOX_HEADS + 3 * CONV_W + 4 * RET_W + N_BRANCH * D_MODEL

kernel_name = "hybrid_fox_shortconv_retention_gated_merge"


def rmsnorm(x, g):
    xf = x.astype(jnp.float32)
    y = xf * lax.rsqrt(jnp.mean(xf * xf, axis=-1, keepdims=True) + EPS)
    return (y * g.astype(jnp.float32)).astype(x.dtype)


def causal_dwconv(u, w):
    k_len = w.shape[0]
    s = u.shape[1]
    up = jnp.pad(u, ((0, 0), (k_len - 1, 0), (0, 0)))
    return sum(up[:, k:k + s] * w[k] for k in range(k_len))


def split_heads(t, n):
    b, s, _ = t.shape
    return t.reshape(b, s, n, -1).transpose(0, 2, 1, 3)


def merge_heads(t):
    b, n, s, d = t.shape
    return t.transpose(0, 2, 1, 3).reshape(b, s, n * d)


def rotary(t, pos):
    half = t.shape[-1] // 2
    inv = ROPE_BASE ** (-jnp.arange(half, dtype=jnp.float32) / half)
    ang = pos.astype(jnp.float32)[:, None] * inv[None, :]
    cos, sin = jnp.cos(ang), jnp.sin(ang)
    t1 = t[..., :half].astype(jnp.float32)
    t2 = t[..., half:].astype(jnp.float32)
    return jnp.concatenate([t1 * cos - t2 * sin, t1 * sin + t2 * cos], axis=-1).astype(t.dtype)


def forgetting_attention(q, k, v, f_logit):
    s_len = q.shape[2]
    scale = q.shape[-1] ** -0.5
    c = jnp.cumsum(jax.nn.log_sigmoid(f_logit.astype(jnp.float32)), axis=-1)
    outs = []
    for i in range(s_len // BLOCK):
        q0, q1 = i * BLOCK, (i + 1) * BLOCK
        sc = jnp.einsum('bhqd,bhkd->bhqk', q[:, :, q0:q1], k[:, :, :q1],
                        preferred_element_type=jnp.float32) * scale
        sc = sc + c[:, :, q0:q1, None] - c[:, :, None, :q1]
        causal = jnp.arange(q1)[None, :] <= jnp.arange(q0, q1)[:, None]
        sc = jnp.where(causal, sc, -jnp.inf)
        p = jax.nn.softmax(sc, axis=-1).astype(v.dtype)
        outs.append(jnp.einsum('bhqk,bhkd->bhqd', p, v[:, :, :q1]))
    return jnp.concatenate(outs, axis=2)


def retention(q, k, v, gammas):
    b, h, s_len, dk = q.shape
    dv = v.shape[-1]
    n_chunk = s_len // BLOCK
    qf = q.astype(jnp.float32).reshape(b, h, n_chunk, BLOCK, dk) * (dk ** -0.5)
    kf = k.astype(jnp.float32).reshape(b, h, n_chunk, BLOCK, dk)
    vf = v.astype(jnp.float32).reshape(b, h, n_chunk, BLOCK, dv)
    log_g = jnp.log(gammas)
    idx = jnp.arange(BLOCK, dtype=jnp.float32)
    diff = idx[:, None] - idx[None, :]
    dmask = jnp.where(diff >= 0, jnp.exp(log_g[:, None, None] * jnp.maximum(diff, 0.0)), 0.0)
    att = jnp.einsum('bhnid,bhnjd->bhnij', qf, kf) * dmask[None, :, None]
    intra = jnp.einsum('bhnij,bhnje->bhnie', att, vf)
    k_dec = kf * jnp.exp(log_g[:, None] * (BLOCK - 1 - idx))[None, :, None, :, None]
    contrib = jnp.einsum('bhncd,bhnce->bhnde', k_dec, vf)
    chunk_decay = jnp.exp(log_g * BLOCK)[None, :, None, None]

    def step(state, u):
        return chunk_decay * state + u, state

    _, s_prev = lax.scan(step, jnp.zeros((b, h, dk, dv), jnp.float32), jnp.moveaxis(contrib, 2, 0))
    s_prev = jnp.moveaxis(s_prev, 0, 2)
    q_dec = qf * jnp.exp(log_g[:, None] * (idx + 1.0))[None, :, None, :, None]
    cross = jnp.einsum('bhncd,bhnde->bhnce', q_dec, s_prev)
    return (intra + cross).reshape(b, h, s_len, dv)


def head_groupnorm(y, g, b):
    mu = jnp.mean(y, axis=-1, keepdims=True)
    var = jnp.mean(jnp.square(y - mu), axis=-1, keepdims=True)
    yn = merge_heads((y - mu) * lax.rsqrt(var + EPS))
    return yn * g.astype(jnp.float32) + b.astype(jnp.float32)


def memory_cross_attention(h, mem_n, w_cq, w_ckv, w_co):
    q = split_heads(h @ w_cq, CROSS_HEADS)
    k, v = jnp.split(mem_n @ w_ckv, 2, axis=-1)
    k = split_heads(k, CROSS_HEADS)
    v = split_heads(v, CROSS_HEADS)
    sc = jnp.einsum('bhqd,bhkd->bhqk', q, k, preferred_element_type=jnp.float32) * (HEAD_DIM ** -0.5)
    p = jax.nn.softmax(sc, axis=-1).astype(v.dtype)
    return merge_heads(jnp.einsum('bhqk,bhkd->bhqd', p, v)) @ w_co


def conv_glu_ffn(h, w_up, conv_w, conv_b, w_down):
    u = causal_dwconv(h @ w_up, conv_w) + conv_b
    a, g = jnp.split(u, 2, axis=-1)
    return (jax.nn.silu(g) * a) @ w_down


def setup_inputs(seed: int = 0) -> dict:
    key = jax.random.key(seed)
    ks = jax.random.split(key, 26)
    L = DEPTH

    def nrm(k, shape, fan_in):
        return jax.random.normal(k, shape, jnp.float32) * (fan_in ** -0.5)

    def gain(k, shape):
        return 1.0 + 0.02 * jax.random.normal(k, shape, jnp.float32)

    def small(k, shape):
        return 0.01 * jax.random.normal(k, shape, jnp.float32)

    b_f = (jnp.linspace(1.0, 5.0, FOX_HEADS, dtype=jnp.float32)[None, :]
           + 0.1 * jax.random.normal(ks[3], (L, FOX_HEADS), jnp.float32))
    return {
        "x": jax.random.normal(ks[0], (BATCH, SEQ, D_MODEL), jnp.float32),
        "mem": jax.random.normal(ks[1], (BATCH, MEM_LEN, D_MODEL), jnp.float32),
        "g_mix": gain(ks[2], (L, D_MODEL)),
        "w_in": nrm(ks[4], (L, D_MODEL, N_IN), D_MODEL),
        "b_f": b_f,
        "b_gate": small(ks[5], (L, N_BRANCH * D_MODEL)),
        "conv_w": nrm(ks[6], (L, CONV_K, CONV_W), CONV_K),
        "ret_gn_g": gain(ks[7], (L, RET_W)),
        "ret_gn_b": small(ks[8], (L, RET_W)),
        "w_fox_o": nrm(ks[9], (L, FOX_W, D_MODEL), FOX_W),
        "w_conv_o": nrm(ks[10], (L, CONV_W, D_MODEL), CONV_W),
        "w_ret_o": nrm(ks[11], (L, RET_W, D_MODEL), RET_W),
        "w_out": nrm(ks[12], (L, D_MODEL, D_MODEL), D_MODEL),
        "g_cross": gain(ks[13], (L, D_MODEL)),
        "g_mem": gain(ks[14], (L, D_MODEL)),
        "w_cq": nrm(ks[15], (L, D_MODEL, CROSS_W), D_MODEL),
        "w_ckv": nrm(ks[16], (L, D_MODEL, 2 * CROSS_W), D_MODEL),
        "w_co": nrm(ks[17], (L, CROSS_W, D_MODEL), CROSS_W),
        "g_ffn": gain(ks[18], (L, D_MODEL)),
        "w_up": nrm(ks[19], (L, D_MODEL, 2 * D_FF), D_MODEL),
        "ffn_conv_w": nrm(ks[20], (L, CONV_K, 2 * D_FF), CONV_K),
        "ffn_conv_b": small(ks[21], (L, 2 * D_FF)),
        "w_down": nrm(ks[22], (L, D_FF, D_MODEL), D_FF),
        "g_final": gain(ks[23], (D_MODEL,)),
    }


def reference(x, mem, g_mix, w_in, b_f, b_gate, conv_w, ret_gn_g, ret_gn_b, w_fox_o, w_conv_o,
              w_ret_o, w_out, g_cross, g_mem, w_cq, w_ckv, w_co, g_ffn, w_up, ffn_conv_w,
              ffn_conv_b, w_down, g_final):
    s_len = x.shape[1]
    pos = jnp.arange(s_len)
    gammas = 1.0 - 2.0 ** (-5.0 - jnp.arange(RET_HEADS, dtype=jnp.float32))
    sizes = [FOX_W] * 3 + [FOX_HEADS] + [CONV_W] * 3 + [RET_W] * 4 + [D_MODEL] * N_BRANCH
    cuts = [int(c) for c in np.cumsum(sizes)[:-1]]
    for l in range(DEPTH):
        h = rmsnorm(x, g_mix[l])
        z = h @ w_in[l]
        (fq, fk, fv, fl, cb, cc, ch, rq, rk, rv, rg, ga, gb, gc) = jnp.split(z, cuts, axis=-1)

        f_logit = (fl + b_f[l]).transpose(0, 2, 1)
        y_a = merge_heads(forgetting_attention(split_heads(fq, FOX_HEADS), split_heads(fk, FOX_HEADS),
                                               split_heads(fv, FOX_HEADS), f_logit)) @ w_fox_o[l]

        y_b = (cb * causal_dwconv(cc * ch, conv_w[l])) @ w_conv_o[l]

        r = retention(rotary(split_heads(rq, RET_HEADS), pos), rotary(split_heads(rk, RET_HEADS), pos),
                      split_heads(rv, RET_HEADS), gammas)
        r = head_groupnorm(r, ret_gn_g[l], ret_gn_b[l]).astype(x.dtype)
        y_c = (jax.nn.silu(rg) * r) @ w_ret_o[l]

        gates = jax.nn.sigmoid(jnp.concatenate([ga, gb, gc], axis=-1) + b_gate[l])
        g_a, g_b, g_c = jnp.split(gates, N_BRANCH, axis=-1)
        x = x + (g_a * y_a + g_b * y_b + g_c * y_c) @ w_out[l]

        x = x + memory_cross_attention(rmsnorm(x, g_cross[l]), rmsnorm(mem, g_mem[l]),
                                       w_cq[l], w_ckv[l], w_co[l])

        x = x + conv_glu_ffn(rmsnorm(x, g_ffn[l]), w_up[l], ffn_conv_w[l], ffn_conv_b[l], w_down[l])
    return rmsnorm(x, g_final)
```

```python
import contextlib
import numpy as np
import concourse.bass as bass
import concourse.mybir as mybir

F32 = mybir.dt.float32
BF16 = mybir.dt.bfloat16
AF = mybir.ActivationFunctionType
ALU = mybir.AluOpType
AX = mybir.AxisListType

ENGS = ("pe", "act", "dve", "pool", "sp")


class Buf:
    __slots__ = ("name", "last_write", "readers")

    def __init__(self, name):
        self.name = name
        self.last_write = None
        self.readers = []


class Op:
    __slots__ = ("eng", "fn", "deps", "is_dma", "signal", "count", "sem", "semval", "idx", "prewait", "inc")

    def __init__(self, eng, fn, is_dma):
        self.eng = eng
        self.fn = fn
        self.deps = []
        self.is_dma = is_dma
        self.signal = False
        self.count = None
        self.sem = None
        self.semval = None
        self.prewait = None


class Prog:
    N_DMA_SEMS = 56
    SEM_CLASSES = {"sp": (0, 20), "act": (20, 12), "pool": (32, 12), "cc": (44, 12)}

    def __init__(self):
        self.nc = bass.Bass("TRN2", target_bir_lowering=False)
        self.stack = contextlib.ExitStack()
        self.ops = {e: [] for e in ENGS}
        self.dma_rr = {k: 0 for k in self.SEM_CLASSES}
        self.dma_sem_use = [0] * self.N_DMA_SEMS
        self.dma_sem_last = [None] * self.N_DMA_SEMS
        self.dma_sem_bar = [0] * self.N_DMA_SEMS
        self.bg = False
        self.nbuf = 0
        self.sb_bytes = 0

    def sbuf(self, name, shape, dt):
        t = self.stack.enter_context(self.nc.sbuf_tensor(name, list(shape), dt))
        n = 1
        for s in shape[1:]:
            n *= s
        self.sb_bytes += n * mybir.dt.size(dt)
        return t

    def psum(self, name, shape, dt=F32):
        return self.stack.enter_context(self.nc.psum_tensor(name, list(shape), dt))

    def dram(self, name, shape, dt, kind="Internal"):
        if kind == "Internal":
            return self.nc.dram_tensor(name, list(shape), dt)
        return self.nc.dram_tensor(name, list(shape), dt, kind=kind)

    def buf(self, name=None):
        self.nbuf += 1
        return Buf(name or f"b{self.nbuf}")

    def _record(self, eng, fn, reads, writes, is_dma=False, inc=16):
        op = Op(eng, fn, is_dma)
        op.inc = inc
        deps = []
        for r in reads:
            if r.last_write is not None:
                deps.append(r.last_write)
        for w in writes:
            if w.last_write is not None:
                deps.append(w.last_write)
            deps.extend(w.readers)
        seen = set()
        for d in deps:
            if d is op or id(d) in seen:
                continue
            seen.add(id(d))
            if d.eng == "pe" and eng == "pe" and not d.is_dma:
                continue
            op.deps.append(d)
            if not d.is_dma:
                d.signal = True
        for w in writes:
            w.last_write = op
            w.readers = []
        for r in reads:
            r.readers.append(op)
        if is_dma:
            cls = "cc" if inc == 1 else eng
            base, cnt_ = self.SEM_CLASSES[cls]
            s = base + self.dma_rr[cls] % cnt_
            self.dma_rr[cls] += 1
            op.prewait = self.dma_sem_last[s]
            self.dma_sem_use[s] += inc
            op.sem = s
            op.semval = self.dma_sem_use[s]
            self.dma_sem_last[s] = op
            if cls != "cc" and not self.bg:
                self.dma_sem_bar[s] = op.semval
        self.ops[eng].append(op)
        return op

    pool_alt = None

    def op(self, eng, fn, reads=(), writes=()):
        if eng == "pool" and self.pool_alt:
            eng = self.pool_alt
        return self._record(eng, fn, reads, writes, False)

    def dma(self, eng, out, in_, reads=(), writes=(), **kw):
        return self._record(eng, lambda e: e.dma_start(out=out, in_=in_, **kw), reads, writes, True)

    def custom_dma(self, eng, fn, reads=(), writes=()):
        return self._record(eng, fn, reads, writes, True)

    def emit(self, final_waits=()):
        nc = self.nc
        esem = {e: self.stack.enter_context(nc.semaphore(f"s_{e}")) for e in ENGS}
        dsem = [self.stack.enter_context(nc.semaphore(f"d_{i}")) for i in range(self.N_DMA_SEMS)]
        for e in ENGS:
            c = 0
            for op in self.ops[e]:
                if op.signal and not op.is_dma:
                    c += 1
                    op.count = c
        block = self.stack.enter_context(nc.Block())
        ninst = 0

        def run(e, eng):
            waited_e = {x: 0 for x in ENGS}
            waited_d = [0] * self.N_DMA_SEMS
            n = 0
            for op in self.ops[e]:
                if op.fn is None:
                    for (s_, v_) in op.prewait:
                        if waited_d[s_] < v_:
                            eng.wait_ge(dsem[s_], v_)
                            waited_d[s_] = v_
                            n += 1
                    for d in op.deps:
                        if d.eng == e:
                            continue
                        if waited_e[d.eng] < d.count:
                            eng.wait_ge(esem[d.eng], d.count)
                            waited_e[d.eng] = d.count
                            n += 1
                    continue
                if op.is_dma and op.prewait is not None:
                    p = op.prewait
                    if waited_d[p.sem] < p.semval:
                        eng.wait_ge(dsem[p.sem], p.semval)
                        waited_d[p.sem] = p.semval
                        n += 1
                need_e = {}
                need_d = {}
                for d in op.deps:
                    if d.is_dma:
                        if need_d.get(d.sem, 0) < d.semval:
                            need_d[d.sem] = d.semval
                    else:
                        if need_e.get(d.eng, 0) < d.count:
                            need_e[d.eng] = d.count
                for s_, v_ in need_d.items():
                    if waited_d[s_] < v_:
                        eng.wait_ge(dsem[s_], v_)
                        waited_d[s_] = v_
                        n += 1
                for e_, v_ in need_e.items():
                    if waited_e[e_] < v_:
                        eng.wait_ge(esem[e_], v_)
                        waited_e[e_] = v_
                        n += 1
                ins = op.fn(eng)
                n += 1
                if op.is_dma:
                    ins.then_inc(dsem[op.sem], op.inc)
                elif op.signal:
                    ins.then_inc(esem[e], 1)
            last = {}
            for op in self.ops[e]:
                if op.is_dma:
                    last[op.sem] = op.semval
            for s, v in last.items():
                if waited_d[s] < v:
                    eng.wait_ge(dsem[s], v)
            return n

        counts = {}

        @block.tensor
        def _(t):
            counts["pe"] = run("pe", t)

        @block.scalar
        def _(s):
            counts["act"] = run("act", s)

        @block.vector
        def _(v):
            counts["dve"] = run("dve", v)

        @block.gpsimd
        def _(g):
            counts["pool"] = run("pool", g)

        @block.sync
        def _(sp):
            counts["sp"] = run("sp", sp)

        self.stack.close()
        return counts


def _barrier(self):
    lasts = []
    for e in ENGS:
        for op in reversed(self.ops[e]):
            if not op.is_dma and op.fn is not None:
                op.signal = True
                lasts.append(op)
                break
    snap = [(s, v) for s, v in enumerate(self.dma_sem_bar) if v > 0]
    for e in ENGS:
        b = Op(e, None, False)
        b.deps = list(lasts)
        b.prewait = snap
        self.ops[e].append(b)


Prog.barrier = _barrier


def MM(P, out, lhsT, rhs, start, stop, reads, writes, **kw):
    return P.op("pe", lambda e: e.matmul(out, lhsT, rhs, start=start, stop=stop, **kw), reads, writes)


def TR(P, out, in_, ident, reads, writes):
    return P.op("pe", lambda e: e.transpose(out, in_, ident), reads, writes)


def ACTV(P, out, in_, func, reads, writes, bias=None, scale=None, accum_out=None, eng="act"):
    kw = {}
    if bias is not None:
        kw["bias"] = bias
    if scale is not None:
        kw["scale"] = scale
    if accum_out is not None:
        kw["accum_out"] = accum_out
    return P.op(eng, lambda e: e.activation(out, in_, func, **kw), reads, writes)


def TS(P, eng, out, in0, s1, s2, op0, op1, reads, writes):
    if op1 is None:
        return P.op(eng, lambda e: e.tensor_scalar(out, in0, s1, s2, op0), reads, writes)
    return P.op(eng, lambda e: e.tensor_scalar(out, in0, s1, s2, op0, op1), reads, writes)


def STT(P, eng, out, in0, scalar, in1, op0, op1, reads, writes):
    return P.op(eng, lambda e: e.scalar_tensor_tensor(out, in0, scalar, in1, op0, op1), reads, writes)


def TT(P, eng, out, in0, in1, op, reads, writes):
    return P.op(eng, lambda e: e.tensor_tensor(out, in0, in1, op), reads, writes)


def CP(P, eng, out, in_, reads, writes):
    if eng == "act":
        return P.op(eng, lambda e: e.copy(out, in_), reads, writes)
    return P.op(eng, lambda e: e.tensor_copy(out, in_), reads, writes)


def MEMSET(P, eng, ap, val, writes):
    return P.op(eng, lambda e: e.memset(ap, val), (), writes)


import numpy as np
import ml_dtypes
from concourse.bass_utils import run_bass_kernel_spmd

T = 512
HD = 128
NH = 8
HW = NH * HD
EPS = 1e-6
NEG = -30000.0


class Cfg:
    def __init__(self, B=4, S=8192, D=2048, DFF=5632, L=2, MEM=256):
        self.B, self.S, self.D, self.DFF, self.L, self.MEM = B, S, D, DFF, L, MEM
        self.KD = D // 128
        self.NBo = S // (2 * T)
        self.TO = self.NBo * T
        self.FC = DFF // 128
        self.NIN = 3 * HW + NH + 3 * HW + 4 * HW + 3 * D
        o = {}
        off = 0
        for nm, sz in [("fq", HW), ("fk", HW), ("fv", HW), ("fl", NH), ("cb", HW), ("cc", HW), ("ch", HW),
                       ("rq", HW), ("rk", HW), ("rv", HW), ("rg", HW), ("ga", D), ("gb", D), ("gc", D)]:
            o[nm] = off
            off += sz
        self.o = o
        self.NBS = min(4, self.NBo)
        self.NCH = S // 128
        self.NHALO = self.NBo * 2


def vec_layout(cfg):
    lay = {}
    off = 0

    def add(nm, ncol):
        nonlocal off
        lay[nm] = off
        off += ncol

    for l in range(cfg.L):
        add(("g_mix", l), cfg.KD)
        add(("b_gate", l), 3 * cfg.KD)
        add(("conv_w", l), 3 * NH)
        add(("gn_g", l), NH)
        add(("gn_b", l), NH)
        add(("b_f", l), 1)
        add(("nb_f", l), 1)
        add(("g_cross", l), cfg.KD)
        add(("g_mem", l), cfg.KD)
        add(("g_ffn", l), cfg.KD)
        add(("fcw", l), 3 * 2 * cfg.FC)
        add(("fcb", l), 2 * cfg.FC)
    add("g_final", cfg.KD)
    add("pf", 1)
    add("omp", 1)
    add("eps", 1)
    add("one", 1)
    return lay, off


def colmajor(v):
    v = np.asarray(v, np.float32)
    return np.ascontiguousarray(v.reshape(-1, 128).T)


def build_vecs(cfg, inp, p):
    lay, nv = vec_layout(cfg)
    V = np.zeros((128, nv), np.float32)

    def put(key, arr2d):
        V[:, lay[key]:lay[key] + arr2d.shape[1]] = arr2d

    for l in range(cfg.L):
        put(("g_mix", l), colmajor(inp["g_mix"][l]))
        put(("b_gate", l), colmajor(inp["b_gate"][l]))
        cw = np.asarray(inp["conv_w"][l], np.float32)
        put(("conv_w", l), np.concatenate([colmajor(cw[k]) for k in range(3)], axis=1))
        put(("gn_g", l), colmajor(inp["ret_gn_g"][l]))
        put(("gn_b", l), colmajor(inp["ret_gn_b"][l]))
        bf = np.zeros((128, 1), np.float32)
        bf[:NH, 0] = np.asarray(inp["b_f"][l], np.float32)
        put(("b_f", l), bf)
        put(("nb_f", l), bf)
        put(("g_cross", l), colmajor(inp["g_cross"][l]))
        put(("g_mem", l), colmajor(inp["g_mem"][l]))
        put(("g_ffn", l), colmajor(inp["g_ffn"][l]))
        fw_ = np.asarray(inp["ffn_conv_w"][l], np.float32)
        put(("fcw", l), np.concatenate([colmajor(fw_[k]) for k in range(3)], axis=1))
        put(("fcb", l), colmajor(inp["ffn_conv_b"][l]))
    put("g_final", colmajor(inp["g_final"]))
    V[:, lay["pf"]] = float(p)
    V[:, lay["omp"]] = 1.0 - float(p)
    V[:, lay["eps"]] = EPS
    V[:, lay["one"]] = 1.0
    return V


def const_tables(cfg, p):
    S, NBo = cfg.S, cfg.NBo
    half = HD // 2
    inv = 10000.0 ** (-np.arange(half, dtype=np.float32) / half)
    pos = np.concatenate([np.arange((2 * i + p) * T, (2 * i + p + 1) * T) for i in range(NBo)]).astype(np.float32)
    ang = pos[:, None] * inv[None, :]
    cos = np.cos(ang).astype(np.float32).T
    sin = np.sin(ang).astype(np.float32).T
    cosT = np.concatenate([cos, cos], 0)
    sinT = np.concatenate([sin, sin], 0)
    rope = np.stack([cosT, sinT], 0)
    gam = 1.0 - 2.0 ** (-5.0 - np.arange(NH, dtype=np.float32))
    lg = np.log(gam.astype(np.float32)).astype(np.float32)
    idx = np.arange(128, dtype=np.float32)
    scale = HD ** -0.5
    dq = np.exp(lg[:, None] * (idx[None, :] + 1.0)).astype(np.float32) * scale
    decq = np.broadcast_to(np.tile(dq, (1, T // 128))[:, None, :], (NH, 128, T)).astype(np.float32)
    diff = idx[None, :] - idx[:, None]
    dm = np.where(diff >= 0, np.exp(lg[:, None, None] * np.maximum(diff, 0.0)[None]), 0.0).astype(np.float32)
    dk = np.exp(lg[:, None] * (127.0 - idx)[None, :]).astype(np.float32)
    dkT = np.ascontiguousarray(dk.T)
    cdec = np.exp(lg * 128.0).astype(np.float32)
    tk = np.arange(128)[:, None]
    tq = np.arange(T)[None, :]
    masks = np.zeros((8, 128, T), np.float32)
    for r in range(8):
        if p == 0:
            if r < 4:
                masks[r] = np.where(r * 128 + tk <= tq, 0.0, NEG)
            else:
                masks[r] = NEG
        else:
            if r < 4:
                masks[r] = 0.0
            else:
                masks[r] = np.where((r - 4) * 128 + tk <= tq, 0.0, NEG)
    pm = np.zeros((128, 128), np.float32)
    for m in range(128):
        if m < 64:
            pm[m + 64, m] = -1.0
        else:
            pm[m - 64, m] = 1.0
    return dict(rope=rope, decq=decq, dmaskT=dm, dkT=dkT, cdec=cdec,
                masks=masks.astype(ml_dtypes.bfloat16), pm=pm,
                ident=np.eye(128, dtype=np.float32))


def x_to_blocks(cfg, xb, p):
    out = np.empty((cfg.NBo, 128, cfg.KD * T), np.float32)
    for i in range(cfg.NBo):
        J = 2 * i + p
        blk = xb[J * T:(J + 1) * T, :]
        out[i] = blk.T.reshape(cfg.KD, 128, T).transpose(1, 0, 2).reshape(128, cfg.KD * T)
    return out


def blocks_to_x(cfg, blocks, p, xb_out):
    for i in range(cfg.NBo):
        J = 2 * i + p
        blk = blocks[i].reshape(128, cfg.KD, T).transpose(1, 0, 2).reshape(cfg.D, T)
        xb_out[J * T:(J + 1) * T, :] = blk.T


def xpred_host(cfg, xb, p):
    out = np.zeros((128, cfg.KD, cfg.NBo, 2), np.float32)
    for i in range(cfg.NBo):
        J = 2 * i + p
        if J == 0:
            continue
        tail = xb[J * T - 2:J * T, :]
        out[:, :, i, :] = tail.T.reshape(cfg.KD, 128, 2).transpose(1, 0, 2)
    return out.reshape(128, -1)


class MK:
    def __init__(self, cfg, taps=()):
        self.cfg = cfg
        self.taps = set(taps)
        self.P = Prog()
        P = self.P
        cfg_ = cfg
        KD, NBo, TO = cfg.KD, cfg.NBo, cfg.TO
        self.lay, self.NV = vec_layout(cfg)
        self.xT = P.dram("xT", [NBo, 128, KD * T], F32, kind="ExternalInput")
        self.out = P.dram("out", [NBo, 128, KD * T], F32, kind="ExternalOutput")
        self.xpred0 = P.dram("xpred0", [128, KD * NBo * 2], F32, kind="ExternalInput")
        self.memT = P.dram("memT", [128, KD * cfg.MEM], F32, kind="ExternalInput")
        self.vecs_d = P.dram("vecs", [128, self.NV], F32, kind="ExternalInput")
        self.rope_d = P.dram("rope", [2, 128, TO], F32, kind="ExternalInput")
        self.decq_d = P.dram("decq", [NH, 128, T], F32, kind="ExternalInput")
        self.dmaskT_d = P.dram("dmaskT", [NH, 128, 128], F32, kind="ExternalInput")
        self.dkT_d = P.dram("dkT", [128, NH], F32, kind="ExternalInput")
        self.masks_d = P.dram("masks", [8, 128, T], BF16, kind="ExternalInput")
        self.pm_d = P.dram("pm", [128, 128], F32, kind="ExternalInput")
        self.ident_d = P.dram("ident", [128, 128], F32, kind="ExternalInput")
        self.W = {}
        self.vecs = P.sbuf("vecs_sb", [128, self.NV], F32)
        self.b_vecs = P.buf("vecs")
        self.onesD = P.sbuf("onesD", [128, 128], BF16)
        self.ones = P.sbuf("ones", [128, 128], BF16)
        self.ones_f = P.sbuf("ones_f", [128, 128], F32)
        self.o128 = P.sbuf("o128", [128, 128], BF16)
        self.ident_f = P.sbuf("ident_f", [128, 128], F32)
        self.ident_b = P.sbuf("ident_b", [128, 128], BF16)
        self.pm_f = P.sbuf("pm_f", [128, 128], F32)
        self.masks = P.sbuf("masks_sb", [128, 8, T], BF16)
        self.b_const = P.buf("const")
        P.dma("sp", self.vecs[:], self.vecs_d[:, :], writes=[self.b_vecs])
        P.dma("sp", self.ident_f[:], self.ident_d[:, :], writes=[self.b_const])
        P.dma("sp", self.pm_f[:], self.pm_d[:, :], writes=[self.b_const])
        P.dma("sp", self.masks[:], self.masks_d.ap().rearrange("r p t -> p r t"), writes=[self.b_const])
        MEMSET(P, "dve", self.onesD[:], 1.0 / cfg.D, [self.b_const])
        MEMSET(P, "dve", self.ones[:], 1.0, [self.b_const])
        MEMSET(P, "dve", self.ones_f[:], 1.0, [self.b_const])
        MEMSET(P, "dve", self.o128[:], 1.0 / 128.0, [self.b_const])
        CP(P, "dve", self.ident_b[:], self.ident_f[:], [self.b_const], [self.b_const])
        for l in range(cfg.L):
            c = self.lay[("nb_f", l)]
            TS(P, "dve", self.vecs[:, c:c + 1], self.vecs[:, c:c + 1], -1.0, None, ALU.mult, None,
               [self.b_vecs], [self.b_vecs])
        self.ps = [P.psum(f"ps{i}", [128, T], F32) for i in range(8)]
        self.b_ps = [P.buf(f"ps{i}") for i in range(8)]
        self.ABYTES = 176 * 1024
        self.arena = P.sbuf("arena", [128, self.ABYTES // 2], BF16)
        self.abump = 0
        self.alloc_scratch()

    def weight_shapes(self):
        c = self.cfg
        return dict(w_in=(c.D, c.NIN), w_fox_o=(HW, c.D), w_conv_o=(HW, c.D), w_ret_o=(HW, c.D),
                    w_out=(c.D, c.D), w_cq=(c.D, 512), w_ckv=(c.D, 1024), w_co=(512, c.D),
                    w_up=(c.D, 2 * c.DFF), w_down=(c.DFF, c.D))

    def v(self, key, ncol=1, off=0):
        c = self.lay[key] + off
        return self.vecs[:, c:c + ncol]

    def dr(self, name, shape, dt):
        kind = "ExternalOutput" if name in self.taps else "Internal"
        return self.P.dram(name, shape, dt, kind=kind)

    def alloc_scratch(self):
        c = self.cfg
        P = self.P
        NBo, TO = c.NBo, c.TO
        self.qT = self.dr("qT", [NH, NBo, 128, T], BF16)
        self.rqT = self.dr("rqT", [NH, NBo, 128, T], BF16)
        self.rqdT = self.dr("rqdT", [NH, NBo, 128, T], BF16)
        self.rgT = self.dr("rgT", [NH, NBo, 128, T], BF16)
        self.convT = self.dr("convT", [NH, NBo, 128, T], BF16)
        self.gates = self.dr("gates", [3 * c.KD, NBo, 128, T], BF16)
        self.attnT = self.dr("attnT", [NH, NBo, 128, T], BF16)
        self.retoT = self.dr("retoT", [NH, NBo, 128, T], BF16)
        self.XR = NH * 128 * NBo
        self.RC = min(2048, self.XR)
        self.NCHK = 4 * self.XR // self.RC
        self.HPC = self.RC // (128 * NBo)
        self.xch_own = P.dram("xch_own", [4 * self.XR, T], BF16)
        self.xch_all = P.dram("xch_all", [self.NCHK, 2 * self.RC, T], BF16)
        self.b_xallk = [P.buf() for _ in range(self.NCHK)]
        self.ls_own = P.dram("ls_own", [NH, TO], F32)
        self.ls_all = P.dram("ls_all", [2 * NH, TO], F32)
        self.cq_d = self.dr("cq_d", [NH, TO], F32)
        self.tail_own = P.dram("tail_own", [128, c.KD * NBo * 2], F32)
        self.tail_all = P.dram("tail_all", [256, c.KD * NBo * 2], F32)
        self.b_q = P.buf(); self.b_rq = P.buf(); self.b_rg = P.buf(); self.b_conv = P.buf()
        self.b_gates = P.buf(); self.b_attn = P.buf(); self.b_reto = P.buf()
        self.b_xown = P.buf(); self.b_xall = P.buf(); self.b_lsown = P.buf(); self.b_lsall = P.buf()
        self.b_cq = P.buf(); self.b_tailown = P.buf(); self.b_tailall = P.buf()
        self.b_x = [[P.buf(f"x{i}_{k}") for k in range(c.KD)] for i in range(NBo)]

    def xch_view(self, which, region, r=None):
        XR = self.XR
        assert which == "own"
        base = self.xch_own.ap()[region * XR:(region + 1) * XR, :]
        return base.rearrange("(h p i) t -> h p i t", h=NH, p=128)

    def xall(self, region, r, h):
        NBo = self.cfg.NBo
        k = (region * self.XR) // self.RC + h // self.HPC
        o = r * self.RC + (h % self.HPC) * 128 * NBo
        ap = self.xch_all.ap()[k, o:o + 128 * NBo, :].rearrange("(p i) t -> p i t", p=128)
        return ap, self.b_xallk[k]

    def pair_gather(self, src_ap, dst_ap, reads, writes):
        groups = [[0, 1], [2, 3], [4, 5], [6, 7]]
        return self.P._record("pool", lambda e: e.collective_compute(
            "AllGather", ALU.bypass, replica_groups=groups, ins=[src_ap.opt()], outs=[dst_ap.opt()]),
            reads, writes, is_dma=True, inc=1)

    def reset_arena(self):
        self.abump = 0
        self.region_bufs = {}

    def ab(self, nbytes):
        o = self.abump
        self.abump += (nbytes + 3) // 4 * 4
        assert self.abump <= self.ABYTES, (self.abump, self.ABYTES)
        return o

    def _shape(self, v, shape):
        if len(shape) == 2:
            return v.rearrange("p (a b) -> p a b", a=shape[0])
        if len(shape) == 3:
            return v.rearrange("p (a b c) -> p a b c", a=shape[0], b=shape[1])
        return v

    def av(self, boff, shape):
        n = int(np.prod(shape))
        assert boff % 2 == 0 and boff + 2 * n <= self.ABYTES
        return self._shape(self.arena[:, boff // 2:boff // 2 + n], shape)

    def fv(self, boff, shape):
        n = int(np.prod(shape))
        assert boff % 4 == 0 and boff + 4 * n <= self.ABYTES
        return self._shape(self.arena[:, boff // 2:boff // 2 + 2 * n].bitcast(F32), shape)

    def abf(self, shape):
        n = int(np.prod(shape))
        return self.av(self.ab(2 * n), shape)

    def af32(self, shape):
        n = int(np.prod(shape))
        return self.fv(self.ab(4 * n), shape)

    def norm_tile(self, xs, b_xs, n, gkey, hT_out, b_h, sq, b_sq, tmp, b_tmp, psb=7):
        P, c = self.P, self.cfg
        KD = c.KD
        ACTV(P, sq, xs, AF.Square, [b_xs], [b_sq])
        ps = self.ps[psb][:, 0:n]
        for k in range(KD):
            MM(P, ps, self.onesD[:], sq[:, k, :], k == 0, k == KD - 1, [b_sq, self.b_const], [self.b_ps[psb]])
        ACTV(P, tmp, ps, AF.Ln, [self.b_ps[psb], self.b_vecs], [b_tmp], bias=self.v("eps"), scale=1.0)
        ACTV(P, tmp, tmp, AF.Exp, [b_tmp], [b_tmp], scale=-0.5)
        for k in range(KD):
            STT(P, "dve", hT_out[:, k, :], xs[:, k, :], self.v(gkey, 1, k), tmp, ALU.mult, ALU.mult,
                [b_xs, b_tmp, self.b_vecs], [b_h])

    def gemm_fm(self, Kc, ntt, groups, ep, wbf_off, banks, halo=None, b_halo=None, nh=0, ep_halo=None, nslot=2, wreads=()):
        P = self.P
        gmax = max(len(g) for g in groups)
        GW = gmax * 128
        wbf = [self.av(wbf_off + s * Kc * GW * 2, [Kc, GW]) for s in range(nslot)]
        b_wbf = [P.buf() for _ in range(nslot)]
        prev = self.region_bufs.get(wbf_off, [])
        self.region_bufs[wbf_off] = b_wbf

        def load(g):
            s = g % nslot
            ents = groups[g]
            j = 0
            while j < len(ents):
                W2d, co, w = ents[j][0], ents[j][1], ents[j][2]
                j2 = j + 1
                tot = w
                while (j2 < len(ents) and ents[j2][0] is W2d and ents[j2][1] == co + tot and w == 128
                       and ents[j2 - 1][2] == 128):
                    tot += ents[j2][2]
                    j2 += 1
                Wv = W2d.rearrange("(c p) n -> p c n", p=128)
                P.dma("sp", wbf[s][:, :, j * 128:j * 128 + tot], Wv[:, :, co:co + tot], reads=list(wreads),
                      writes=[b_wbf[s]] + (prev if g < nslot else []))
                j = j2

        ng = len(groups)
        for g in range(min(nslot - 1, ng)):
            load(g)
        bi = 0
        for g in range(ng):
            s = g % nslot
            if g + nslot - 1 < ng:
                load(g + nslot - 1)
            if halo is not None:
                hb = []
                for j, (W2d, co, w, xT, b_x) in enumerate(groups[g]):
                    ps = self.ps[7][0:w, j * nh:(j + 1) * nh]
                    for k in range(Kc):
                        MM(P, ps, wbf[s][:, k, j * 128:j * 128 + w], halo[:, k, :], k == 0, k == Kc - 1,
                           [b_wbf[s], b_halo], [self.b_ps[7]])
                    hb.append(ps)
                ep_halo(g, hb)
            for tt in range(ntt):
                res = []
                for j, (W2d, co, w, xT, b_x) in enumerate(groups[g]):
                    psb = banks[bi % len(banks)]
                    bi += 1
                    ps = self.ps[psb][0:w, :]
                    for k in range(Kc):
                        MM(P, ps, wbf[s][:, k, j * 128:j * 128 + w], xT[:, k, tt * T:(tt + 1) * T],
                           k == 0, k == Kc - 1, [b_wbf[s], b_x], [self.b_ps[psb]])
                    res.append((psb, w))
                ep(g, tt, res)

    def gemm_tm(self, W2d, Kc, xT, b_x, ntc, col0, ncols, ep, wbf_off, banks, wreads=()):
        P = self.P
        GW = 256
        ng = ncols // GW
        wbf = [self.av(wbf_off + s * Kc * GW * 2, [Kc, GW]) for s in range(2)]
        b_wbf = [P.buf(), P.buf()]
        prev = self.region_bufs.get(wbf_off, [])
        self.region_bufs[wbf_off] = b_wbf
        Wv = W2d.rearrange("(c p) n -> p c n", p=128)

        def load(g):
            s = g % 2
            P.dma("sp", wbf[s][:, :, :], Wv[:, :, col0 + g * GW:col0 + (g + 1) * GW], reads=list(wreads),
                  writes=[b_wbf[s]] + (prev if g < 2 else []))

        load(0)
        bi = 0
        for g in range(ng):
            s = g % 2
            if g + 1 < ng:
                load(g + 1)
            for tc in range(ntc):
                psb = banks[bi % len(banks)]
                bi += 1
                ps = self.ps[psb][:, 0:GW]
                for k in range(Kc):
                    MM(P, ps, xT[:, k, tc * 128:(tc + 1) * 128], wbf[s][:, k, :], k == 0, k == Kc - 1,
                       [b_wbf[s], b_x], [self.b_ps[psb]])
                ep(g, tc, psb)

    def prep_weights(self, sharded):
        P, c = self.P, self.cfg
        self.sharded = sharded
        self.Wb = {}
        self.b_W = {}
        self.shard = {}
        CH = 2048
        self.reset_arena()
        st = [self.af32([CH]) for s in range(3)]
        sb = [self.abf([CH]) for s in range(3)]
        b_st = [P.buf() for _ in range(3)]
        b_sb = [P.buf() for _ in range(3)]
        it = 0
        engs = ["dve", "act", "pool"]
        for nm, (r, cc) in self.weight_shapes().items():
            rows = r // 8 if sharded else r
            self.W[nm] = P.dram(nm, [c.L, rows, cc], F32, kind="ExternalInput")
        for l in range(c.L):
            for nm, (r, cc) in self.weight_shapes().items():
                full = P.dram(f"{nm}_bf{l}", [r, cc], BF16)
                b_full = P.buf()
                self.Wb[(nm, l)] = full
                self.b_W[(nm, l)] = b_full
                rows = r // 8 if sharded else r
                if sharded:
                    shard = P.dram(f"{nm}_sh{l}", [rows, cc], BF16)
                    b_shard = P.buf()
                    self.shard[(nm, l)] = (shard, b_shard)
                    dst_t, b_dst = shard, b_shard
                else:
                    dst_t, b_dst = full, b_full
                src = self.W[nm].ap()[l].rearrange("r c -> (r c)").rearrange("(p n) -> p n", p=128)
                dst = dst_t.ap().rearrange("r c -> (r c)").rearrange("(p n) -> p n", p=128)
                n = rows * cc // 128
                assert rows * cc % 128 == 0
                for o in range(0, n, CH):
                    w = min(CH, n - o)
                    s = it % 3
                    P.dma("sp", st[s][:, 0:w], src[:, o:o + w], writes=[b_st[s]])
                    CP(P, engs[it % 3], sb[s][:, 0:w], st[s][:, 0:w], [b_st[s]], [b_sb[s]])
                    P.dma("act", dst[:, o:o + w], sb[s][:, 0:w], reads=[b_sb[s]], writes=[b_dst])
                    it += 1
        P.barrier()

    def gather_weights(self, l, names=None):
        if not self.sharded:
            return
        P, c = self.P, self.cfg
        G4 = [[0, 1, 2, 3], [4, 5, 6, 7]]
        G2 = [[0, 4], [1, 5], [2, 6], [3, 7]]
        LIM = 4 * 1024 * 1024
        if not hasattr(self, "_nstg"):
            self._nstg = 0

        def new_stg():
            self._nstg += 1
            return P.dram(f"stg_{self._nstg}", [LIM // 2], BF16), P.buf()

        def cc(groups, a_, b_, reads, writes):
            return P._record("pool", lambda e: e.collective_compute(
                "AllGather", ALU.bypass, replica_groups=groups, ins=[a_.opt()], outs=[b_.opt()]),
                reads, writes, is_dma=True, inc=1)

        P.bg = True
        todo = [(nm, r, cc_) for nm, (r, cc_) in self.weight_shapes().items() if names is None or nm in names]
        st1 = {}
        for nm, r, cc_ in todo:
            shard, b_shard = self.shard[(nm, l)]
            rows = r // 8
            half = P.dram(f"{nm}_hf{l}", [r // 2, cc_], BF16)
            b_half = P.buf()
            rc = max(1, (LIM // 4) // (cc_ * 2))
            lst = []
            for k0 in range(0, rows, rc):
                nr = min(rc, rows - k0)
                stg, b_stg = new_stg()
                cc(G4, shard.ap()[k0:k0 + nr, :], stg.ap()[0:4 * nr * cc_].rearrange("(q c) -> q c", c=cc_),
                   [b_shard], [b_stg])
                lst.append((k0, nr, stg, b_stg))
            st1[nm] = (half, b_half, lst)
        st2 = {}
        for nm, r, cc_ in todo:
            half, b_half, lst = st1[nm]
            hv = half.ap().rearrange("(g q) c -> g q c", g=4)
            for k0, nr, stg, b_stg in lst:
                P.dma("pool", hv[:, k0:k0 + nr, :], stg.ap()[0:4 * nr * cc_].rearrange("(g q c) -> g q c", g=4, c=cc_),
                      reads=[b_stg], writes=[b_half])
            rows2 = r // 2
            rc2 = max(1, (LIM // 2) // (cc_ * 2))
            lst2 = []
            for k0 in range(0, rows2, rc2):
                nr = min(rc2, rows2 - k0)
                stg, b_stg = new_stg()
                cc(G2, half.ap()[k0:k0 + nr, :], stg.ap()[0:2 * nr * cc_].rearrange("(q c) -> q c", c=cc_),
                   [b_half], [b_stg])
                lst2.append((k0, nr, stg, b_stg))
            st2[nm] = lst2
        for nm, r, cc_ in todo:
            full, b_full = self.Wb[(nm, l)], self.b_W[(nm, l)]
            fv_ = full.ap().rearrange("(g q) c -> g q c", g=2)
            for k0, nr, stg, b_stg in st2[nm]:
                P.dma("pool", fv_[:, k0:k0 + nr, :], stg.ap()[0:2 * nr * cc_].rearrange("(g q c) -> g q c", g=2, c=cc_),
                      reads=[b_stg], writes=[b_full])
        P.bg = False

    def stage_P(self, l, xsrc, xpred_sb_loader):
        P, c = self.P, self.cfg
        KD, NBo, NBS, NHALO = c.KD, c.NBo, c.NBS, c.NHALO
        Win = self.Wb[("w_in", l)].ap()
        b_Win = self.b_W[("w_in", l)]
        o = c.o
        self.reset_arena()
        hT = self.abf([KD, NBS * T]); b_hT = P.buf("hT")
        WBF_BYTES = 2 * KD * 512 * 2
        wbf_off = self.ab(max(WBF_BYTES, KD * T * 2))
        sq = self.av(wbf_off, [KD, T]); b_sq = P.buf("sq")
        NOST = 8
        ost = [self.abf([T]) for _ in range(NOST)]; b_ost = [P.buf() for _ in range(NOST)]
        hhalo = self.abf([KD, NHALO]); b_hhalo = P.buf("hhalo")
        sqh = self.abf([KD, NHALO]); b_sqh = P.buf()
        xs1 = self.af32([KD, T]); b_xs1 = P.buf()
        xs = [xs1, xs1]; b_xs = [b_xs1, b_xs1]
        xh = self.af32([KD, NHALO]); b_xh = P.buf()
        tmp = self.af32([T]); b_tmp = P.buf()
        NFT = 7
        ft = [self.af32([T + 2]) for _ in range(NFT)]; b_ft = [P.buf() for _ in range(NFT)]
        tab = [self.af32([T]) for _ in range(3)]; b_tab = [P.buf() for _ in range(3)]
        lsb = self.af32([T]); b_lsb = P.buf()
        state = dict(ost=0, ft=0)
        prodh = self.af32([NH, NHALO]); b_prodh = P.buf()

        def next_ost():
            k = state["ost"] % NOST
            state["ost"] += 1
            return ost[k], b_ost[k]

        def next_ft():
            k = state["ft"] % NFT
            state["ft"] += 1
            return ft[k], b_ft[k]

        xpred_sb_loader(xh, b_xh)
        self.norm_tile(xh, b_xh, NHALO, ("g_mix", l), hhalo, b_hhalo, sqh, b_sqh, tmp[:, 0:NHALO], b_tmp)

        kv = self.xch_view("own", 0)
        vv = self.xch_view("own", 1)
        rkv = self.xch_view("own", 2)
        rvv = self.xch_view("own", 3)
        banks = [0, 1, 2, 3, 4, 5]
        scale = HD ** -0.5
        nst = NBo // NBS
        for st in range(nst):
            for bi in range(NBS):
                blk = st * NBS + bi
                s = bi % 2
                P.dma("sp", xs[s], xsrc.ap()[blk].rearrange("p (k t) -> p k t", k=KD), reads=self.b_x[blk],
                      writes=[b_xs[s]])
                self.norm_tile(xs[s], b_xs[s], T, ("g_mix", l), hT[:, :, bi * T:(bi + 1) * T], b_hT, sq, b_sq,
                               tmp, b_tmp)
            P.barrier()

            def blk_of(tt):
                return st * NBS + tt

            def pairs(off):
                return [[(Win, off + (4 * g + j) * 128, 128, hT, b_hT) for j in range(4)] for g in range(NH // 4)]

            def ep_k(g, tt, res):
                for j, (psb, w) in enumerate(res):
                    h = 4 * g + j
                    t_, bt = next_ost()
                    CP(P, "act", t_, self.ps[psb][:, :], [self.b_ps[psb]], [bt])
                    P.dma("act", kv[h, :, blk_of(tt), :], t_, reads=[bt], writes=[self.b_xown])
            self.gemm_fm(KD, NBS, pairs(o["fk"]), ep_k, wbf_off, banks, wreads=[b_Win])

            def mk_ep_v(view):
                def ep_v(g, tc, psb):
                    blk = st * NBS + tc // 4
                    s4 = tc % 4
                    t_, bt = next_ost()
                    CP(P, "act", t_[:, 0:256], self.ps[psb][:, 0:256], [self.b_ps[psb]], [bt])
                    for j in range(2):
                        h = 2 * g + j
                        P.dma("act", view[h, :, blk, s4 * 128:(s4 + 1) * 128], t_[:, j * 128:(j + 1) * 128],
                              reads=[bt], writes=[self.b_xown])
                return ep_v
            self.gemm_tm(Win, KD, hT, b_hT, NBS * 4, o["fv"], HW, mk_ep_v(vv), wbf_off, banks, wreads=[b_Win])
            self.gemm_tm(Win, KD, hT, b_hT, NBS * 4, o["rv"], HW, mk_ep_v(rvv), wbf_off, banks, wreads=[b_Win])

            def ep_fl(g, tt, res):
                psb, w = res[0]
                ACTV(P, lsb[0:NH, :], self.ps[psb][0:NH, :], AF.Exp, [self.b_ps[psb], self.b_vecs], [b_lsb],
                     bias=self.v(("nb_f", l))[0:NH, :], scale=-1.0)
                ACTV(P, lsb[0:NH, :], lsb[0:NH, :], AF.Ln, [b_lsb, self.b_vecs], [b_lsb],
                     bias=self.v("one")[0:NH, :], scale=1.0)
                blk = blk_of(tt)
                P.dma("act", self.ls_own.ap()[:, blk * T:(blk + 1) * T], lsb[0:NH, :], reads=[b_lsb],
                      writes=[self.b_lsown])
            self.gemm_fm(KD, NBS, [[(Win, o["fl"], NH, hT, b_hT)]], ep_fl, wbf_off, banks, wreads=[b_Win])

            def rotary(psb, blk, outs):
                q32, bq = next_ft()
                CP(P, "act", q32[:, 0:T], self.ps[psb][:, :], [self.b_ps[psb]], [bq])
                MM(P, self.ps[6][:, :], self.pm_f[:], q32[:, 0:T], True, True, [bq, self.b_const], [self.b_ps[6]])
                t1, b1 = next_ft()
                TT(P, "pool", t1[:, 0:T], q32[:, 0:T], tab[0], ALU.mult, [bq, b_tab[0]], [b1])
                t2, b2 = next_ft()
                TT(P, "dve", t2[:, 0:T], self.ps[6][:, :], tab[1], ALU.mult, [self.b_ps[6], b_tab[1]], [b2])
                TT(P, "dve", t1[:, 0:T], t1[:, 0:T], t2[:, 0:T], ALU.add, [b1, b2], [b1])
                for dst, kind, bdst in outs:
                    t_, bt = next_ost()
                    if kind == "scale":
                        ACTV(P, t_, t1[:, 0:T], AF.Copy, [b1], [bt], scale=scale)
                    elif kind == "decq":
                        TT(P, "pool", t_, t1[:, 0:T], tab[2], ALU.mult, [b1, b_tab[2]], [bt])
                    else:
                        CP(P, "act", t_, t1[:, 0:T], [b1], [bt])
                    P.dma("sp" if kind == "decq" else "act", dst, t_, reads=[bt], writes=[bdst])

            def load_tabs(blk, h=None):
                P.dma("sp", tab[0], self.rope_d.ap()[0, :, blk * T:(blk + 1) * T], writes=[b_tab[0]])
                P.dma("sp", tab[1], self.rope_d.ap()[1, :, blk * T:(blk + 1) * T], writes=[b_tab[1]])

            def singles(off):
                return [[(Win, off + h * 128, 128, hT, b_hT)] for h in range(NH)]

            def ep_rk(g, tt, res):
                psb, w = res[0]
                blk = blk_of(tt)
                load_tabs(blk)
                rotary(psb, blk, [(rkv[g, :, blk, :], "copy", self.b_xown)])
            self.gemm_fm(KD, NBS, singles(o["rk"]), ep_rk, wbf_off, banks, wreads=[b_Win])

            if st == nst - 1:
                self.exchange(l)

            def ep_q(g, tt, res):
                for j, (psb, w) in enumerate(res):
                    h = 4 * g + j
                    t_, bt = next_ost()
                    ACTV(P, t_, self.ps[psb][:, :], AF.Copy, [self.b_ps[psb]], [bt], scale=scale)
                    P.dma("act", self.qT.ap()[h, blk_of(tt)], t_, reads=[bt], writes=[self.b_q])
            self.gemm_fm(KD, NBS, pairs(o["fq"]), ep_q, wbf_off, banks, wreads=[b_Win])

            def ep_rq(g, tt, res):
                psb, w = res[0]
                blk = blk_of(tt)
                load_tabs(blk)
                P.dma("sp", tab[2], self.decq_d.ap()[g], writes=[b_tab[2]])
                rotary(psb, blk, [(self.rqT.ap()[g, blk], "scale", self.b_rq), (self.rqdT.ap()[g, blk], "decq", self.b_rq)])
            self.gemm_fm(KD, NBS, singles(o["rq"]), ep_rq, wbf_off, banks, wreads=[b_Win])

            def ep_rg(g, tt, res):
                for j, (psb, w) in enumerate(res):
                    h = 4 * g + j
                    t_, bt = next_ost()
                    ACTV(P, t_, self.ps[psb][:, :], AF.Silu, [self.b_ps[psb]], [bt])
                    P.dma("act", self.rgT.ap()[h, blk_of(tt)], t_, reads=[bt], writes=[self.b_rg])
            self.gemm_fm(KD, NBS, pairs(o["rg"]), ep_rg, wbf_off, banks, wreads=[b_Win])

            def ep_gate(g, tt, res):
                for j, (psb, w) in enumerate(res):
                    m = 4 * g + j
                    t_, bt = next_ost()
                    ACTV(P, t_, self.ps[psb][:, :], AF.Sigmoid, [self.b_ps[psb], self.b_vecs], [bt],
                         bias=self.v(("b_gate", l), 1, m), scale=1.0)
                    P.dma("act", self.gates.ap()[m, blk_of(tt)], t_, reads=[bt], writes=[self.b_gates])
            ng = (3 * KD + 3) // 4
            self.gemm_fm(KD, NBS, [[(Win, o["ga"] + (4 * g + j) * 128, 128, hT, b_hT) for j in range(min(4, 3 * KD - 4 * g))]
                                   for g in range(ng)], ep_gate, wbf_off, banks, wreads=[b_Win])


            def ep_conv_halo(g, hb):
                cc_s, bc = next_ft()
                CP(P, "act", cc_s[:, 0:NHALO], hb[1], [self.b_ps[7]], [bc])
                TT(P, "dve", prodh[:, g, :], cc_s[:, 0:NHALO], hb[2], ALU.mult, [bc, self.b_ps[7]], [b_prodh])

            def ep_conv(g, tt, res):
                blk = blk_of(tt)
                (pb, _), (pc, _), (ph, _) = res
                cc_s, bc = next_ft()
                CP(P, "act", cc_s[:, 0:T], self.ps[pc][:, :], [self.b_ps[pc]], [bc])
                prod, bp = next_ft()
                TT(P, "dve", prod[:, 2:T + 2], cc_s[:, 0:T], self.ps[ph][:, :], ALU.mult, [bc, self.b_ps[ph]], [bp])
                CP(P, "pool", prod[:, 0:2], prodh[:, g, 2 * blk:2 * blk + 2], [b_prodh], [bp])
                y, by = next_ft()
                cw = lambda k: self.v(("conv_w", l), 1, k * NH + g)
                ACTV(P, y[:, 0:T], prod[:, 2:T + 2], AF.Copy, [bp, self.b_vecs], [by], scale=cw(2))
                STT(P, "dve", y[:, 0:T], prod[:, 1:T + 1], cw(1), y[:, 0:T], ALU.mult, ALU.add, [bp, by, self.b_vecs], [by])
                STT(P, "dve", y[:, 0:T], prod[:, 0:T], cw(0), y[:, 0:T], ALU.mult, ALU.add, [bp, by, self.b_vecs], [by])
                t_, bt = next_ost()
                TT(P, "dve", t_, y[:, 0:T], self.ps[pb][:, :], ALU.mult, [by, self.b_ps[pb]], [bt])
                P.dma("sp", self.convT.ap()[g, blk], t_, reads=[bt], writes=[self.b_conv])
            grp = [[(Win, o["cb"] + g * 128, 128, hT, b_hT), (Win, o["cc"] + g * 128, 128, hT, b_hT),
                    (Win, o["ch"] + g * 128, 128, hT, b_hT)] for g in range(NH)]
            self.gemm_fm(KD, NBS, grp, ep_conv, wbf_off, banks, halo=hhalo, b_halo=b_hhalo, nh=NHALO,
                         ep_halo=ep_conv_halo, wreads=[b_Win])
            P.barrier()

    def exchange(self, l):
        self.pair_gather(self.ls_own.ap(), self.ls_all.ap(), [self.b_lsown], [self.b_lsall])
        for k in range(self.NCHK):
            self.pair_gather(self.xch_own.ap()[k * self.RC:(k + 1) * self.RC, :], self.xch_all.ap()[k],
                             [self.b_xown], [self.b_xallk[k]])

    def compute_c(self, l):
        P, c = self.P, self.cfg
        S, NBo, NCH = c.S, c.NBo, c.NCH
        self.reset_arena()
        spg = self.af32([S]); b_spg = P.buf()
        csum = self.af32([S]); b_csum = P.buf()
        ones8 = self.af32([T]); b_o8 = P.buf()
        tmpc = self.af32([NBo * T]); b_tmpc = P.buf()
        if not hasattr(self, "negc"):
            self.negc = P.sbuf("negc", [128, NCH * NH], F32)
            self.b_negc = P.buf()
        MEMSET(P, "dve", ones8[0:NH, :], 1.0, [b_o8])
        spv = spg[0:NH, :].rearrange("p (i r t) -> p i r t", r=2, t=T)
        for r in range(2):
            P.dma("sp", spv[:, :, r, :], self.ls_all.ap()[r * NH:(r + 1) * NH, :].rearrange("p (i t) -> p i t", t=T),
                  reads=[self.b_lsall], writes=[b_spg])
        for n in range(S // T):
            init = 0.0 if n == 0 else csum[0:NH, n * T - 1:n * T]
            sl = slice(n * T, (n + 1) * T)
            P.op("dve", (lambda o_, d1, ini: (lambda e: e.tensor_tensor_scan(o_, ones8[0:NH, :], d1, ini, ALU.mult, ALU.add)))(
                csum[0:NH, sl], spg[0:NH, sl], init), [b_spg, b_o8, b_csum], [b_csum])
        assert NCH * NH <= T
        for n in range(NCH):
            MM(P, self.ps[7][:, n * NH:(n + 1) * NH], csum[0:NH, n * 128:(n + 1) * 128], self.ident_f[0:NH, 0:NH],
               True, True, [b_csum, self.b_const], [self.b_ps[7]])
        CP(P, "act", self.negc[:, :], self.ps[7][:, 0:NCH * NH], [self.b_ps[7]], [self.b_negc])
        cv = csum[0:NH, :].rearrange("p (i r t) -> p i r t", r=2, t=T)
        tv = tmpc[0:NH, :].rearrange("p (i t) -> p i t", t=T)
        TS(P, "dve", tv, cv[:, :, 0, :], self.v("omp")[0:NH, :], None, ALU.mult, None, [b_csum, self.b_vecs], [b_tmpc])
        STT(P, "dve", tv, cv[:, :, 1, :], self.v("pf")[0:NH, :], tv, ALU.mult, ALU.add, [b_csum, b_tmpc, self.b_vecs], [b_tmpc])
        TS(P, "dve", tmpc[0:NH, :], tmpc[0:NH, :], -1.0, None, ALU.mult, None, [b_tmpc], [b_tmpc])
        P.dma("sp", self.cq_d.ap()[:, :], tmpc[0:NH, :], reads=[b_tmpc], writes=[self.b_cq])
        P.barrier()

    def stage_A(self, l):
        P, c = self.P, self.cfg
        NBo, NCH = c.NBo, c.NCH
        self.reset_arena()
        kres = [self.abf([2 * NBo * T]) for _ in range(2)]; b_k = [P.buf(), P.buf()]
        vres = [self.abf([NCH, 128]) for _ in range(2)]; b_v = [P.buf(), P.buf()]
        qt = [self.abf([T]) for _ in range(2)]; b_qt = [P.buf(), P.buf()]
        cqb = [self.af32([T]) for _ in range(2)]; b_cqb = [P.buf(), P.buf()]
        NL = 4
        LA = 2
        lg = [self.af32([T]) for _ in range(NL)]; b_lg = [P.buf() for _ in range(NL)]
        pt = [self.abf([T]) for _ in range(NL)]; b_pt = [P.buf() for _ in range(NL)]
        rec = self.af32([T]); b_rec = P.buf()
        ot = [self.abf([T]) for _ in range(2)]; b_ot = [P.buf(), P.buf()]
        its = []
        qi = 0
        for h in range(NH):
            for i in range(NBo):
                nn = 4 * (2 * i + 2)
                for n in range(nn):
                    its.append((h, i, n, nn, qi % 2))
                qi += 1

        def s_phase(k):
            h, i, n, nn, qs = its[k]
            s = h % 2
            if n == 0 and i == 0:
                kview = kres[s].rearrange("p (i r t) -> p i r t", r=2, t=T)
                vview = vres[s].rearrange("p (i r s) e -> p i r (s e)", r=2, s=4)
                for r in range(2):
                    ka, kb_ = self.xall(0, r, h)
                    va, vb_ = self.xall(1, r, h)
                    P.dma("sp", kview[:, :, r, :], ka, reads=[kb_], writes=[b_k[s]])
                    P.dma("sp", vview[:, :, r, :], va, reads=[vb_], writes=[b_v[s]])
            if n == 0:
                P.dma("sp", qt[qs], self.qT.ap()[h, i], reads=[self.b_q], writes=[b_qt[qs]])
                P.dma("sp", cqb[qs], self.cq_d.ap()[h:h + 1, i * T:(i + 1) * T].broadcast_to([128, T]),
                      reads=[self.b_cq], writes=[b_cqb[qs]])
            sb = k % 3
            ls_ = k % NL
            masked = n >= 8 * i
            MM(P, self.ps[sb][:, :], kres[s][:, n * 128:(n + 1) * 128], qt[qs], True, not masked,
               [b_k[s], b_qt[qs]], [self.b_ps[sb]])
            if masked:
                MM(P, self.ps[sb][:, :], self.ident_b[:], self.masks[:, n - 8 * i, :], False, True,
                   [self.b_const], [self.b_ps[sb]])
            STT(P, "dve", lg[ls_], self.ps[sb][:, :], self.negc[:, n * NH + h:n * NH + h + 1], cqb[qs],
                ALU.add, ALU.add, [self.b_ps[sb], self.b_negc, b_cqb[qs]], [b_lg[ls_]])
            ACTV(P, pt[ls_], lg[ls_], AF.Exp, [b_lg[ls_]], [b_pt[ls_]])

        def pv_phase(k):
            h, i, n, nn, qs = its[k]
            s = h % 2
            ls_ = k % NL
            ob, lb = 3 + qs, 5 + qs
            MM(P, self.ps[ob][:, :], vres[s][:, n, :], pt[ls_], n == 0, n == nn - 1, [b_v[s], b_pt[ls_]],
               [self.b_ps[ob]])
            MM(P, self.ps[lb][:, :], self.ones[:], pt[ls_], n == 0, n == nn - 1, [b_pt[ls_], self.b_const],
               [self.b_ps[lb]])
            if n == nn - 1:
                P.op("dve", (lambda o_, i_: (lambda e: e.reciprocal(o_, i_)))(rec, self.ps[lb][:, :]),
                     [self.b_ps[lb]], [b_rec])
                TT(P, "dve", ot[qs], self.ps[ob][:, :], rec, ALU.mult, [self.b_ps[ob], b_rec], [b_ot[qs]])
                P.dma("sp", self.attnT.ap()[h, i], ot[qs], reads=[b_ot[qs]], writes=[self.b_attn])

        for k in range(len(its) + LA):
            if k < len(its):
                s_phase(k)
            if k - LA >= 0:
                pv_phase(k - LA)
        P.barrier()

    def stage_R(self, l):
        P, c = self.P, self.cfg
        NBo, NCH = c.NBo, c.NCH
        self.reset_arena()
        krT = [self.abf([2 * NBo * T]) for _ in range(2)]; b_kr = [P.buf(), P.buf()]
        vres = [self.abf([NCH, 128]) for _ in range(2)]; b_v = [P.buf(), P.buf()]
        kown = [self.abf([NBo, T]) for _ in range(2)]; b_ko = [P.buf(), P.buf()]
        vown = [self.abf([NBo * 4, 128]) for _ in range(2)]; b_vo = [P.buf(), P.buf()]
        dm4 = [self.af32([4, 128]) for _ in range(2)]; b_dm = [P.buf(), P.buf()]
        dk = self.af32([NH]); b_dk = P.buf()
        P.dma("sp", dk, self.dkT_d.ap(), writes=[b_dk])
        Sst = self.af32([128]); b_S = P.buf()
        Sown = self.abf([NBo * 4, 128]); b_So = P.buf()
        NK = 3
        kdec = [self.abf([128]) for _ in range(NK)]; b_kd = [P.buf() for _ in range(NK)]
        qt = [self.abf([T]) for _ in range(2)]; b_qt = [P.buf(), P.buf()]
        qd = [self.abf([T]) for _ in range(2)]; b_qd = [P.buf(), P.buf()]
        rgt = [self.abf([T]) for _ in range(2)]; b_rg = [P.buf(), P.buf()]
        attm = [self.abf([4, 128]) for _ in range(2)]; b_am = [P.buf(), P.buf()]
        rb = self.abf([T]); b_rb = P.buf()
        rsq = self.abf([T]); b_rsq = P.buf()
        mean = self.af32([T]); b_mean = P.buf()
        msq = self.af32([T]); b_msq = P.buf()
        var = self.af32([T]); b_var = P.buf()
        cen = self.af32([T]); b_cen = P.buf()
        yo = [self.abf([T]) for _ in range(2)]; b_yo = [P.buf(), P.buf()]
        gam = 1.0 - 2.0 ** (-5.0 - np.arange(NH, dtype=np.float32))
        cdec = np.exp(np.log(gam.astype(np.float32)).astype(np.float32) * 128.0).astype(np.float32)
        it = 0
        qi = 0
        for h in range(NH):
            s = h % 2
            kview = krT[s].rearrange("p (i r t) -> p i r t", r=2, t=T)
            vview = vres[s].rearrange("p (i r s) e -> p i r (s e)", r=2, s=4)
            for r in range(2):
                ka, kb_ = self.xall(2, r, h)
                va, vb_ = self.xall(3, r, h)
                P.dma("sp", kview[:, :, r, :], ka, reads=[kb_], writes=[b_kr[s]])
                P.dma("sp", vview[:, :, r, :], va, reads=[vb_], writes=[b_v[s]])
            P.dma("sp", kown[s], self.xch_view("own", 2)[h], reads=[self.b_xown], writes=[b_ko[s]])
            P.dma("sp", vown[s].rearrange("p (i s) e -> p i (s e)", s=4), self.xch_view("own", 3)[h],
                  reads=[self.b_xown], writes=[b_vo[s]])
            for q4 in range(4):
                P.dma("sp", dm4[s][:, q4, :], self.dmaskT_d.ap()[h], writes=[b_dm[s]])
            MEMSET(P, "dve", Sst, 0.0, [b_S])

            def tpose(n, itn):
                kk, pb = itn % NK, itn % 3
                MM(P, self.ps[pb][:, 0:128], krT[s][:, n * 128:(n + 1) * 128], self.ident_b[:], True, True,
                   [b_kr[s], self.b_const], [self.b_ps[pb]])
                ACTV(P, kdec[kk], self.ps[pb][:, 0:128], AF.Copy, [self.b_ps[pb], b_dk], [b_kd[kk]], scale=dk[:, h:h + 1])

            def contrib(n, itn):
                kk, cb = itn % NK, 3
                MM(P, self.ps[cb][:, (itn % 4) * 128:(itn % 4 + 1) * 128], kdec[kk], vres[s][:, n, :], True, True,
                   [b_kd[kk], b_v[s]], [self.b_ps[cb]])
                return self.ps[cb][:, (itn % 4) * 128:(itn % 4 + 1) * 128]

            tpose(0, it)
            for n in range(NCH):
                i, r, s4 = n // 8, (n // 4) % 2, n % 4
                slot = i * 4 + s4
                if r == 0:
                    TS(P, "dve", Sown[:, slot, :], Sst, self.v("omp"), None, ALU.mult, None, [b_S, self.b_vecs], [b_So])
                else:
                    STT(P, "dve", Sown[:, slot, :], Sst, self.v("pf"), Sown[:, slot, :], ALU.mult, ALU.add,
                        [b_S, b_So, self.b_vecs], [b_So])
                if n == NCH - 1:
                    break
                if n + 1 < NCH - 1:
                    tpose(n + 1, it + 1)
                cps = contrib(n, it)
                it += 1
                STT(P, "dve", Sst, Sst, float(cdec[h]), cps, ALU.mult, ALU.add, [b_S, self.b_ps[3]], [b_S])
            for i in range(NBo):
                qs = qi % 2
                qi += 1
                P.dma("sp", qt[qs], self.rqT.ap()[h, i], reads=[self.b_rq], writes=[b_qt[qs]])
                P.dma("sp", qd[qs], self.rqdT.ap()[h, i], reads=[self.b_rq], writes=[b_qd[qs]])
                P.dma("sp", rgt[qs], self.rgT.ap()[h, i], reads=[self.b_rg], writes=[b_rg[qs]])
                ab = 4
                for s4 in range(4):
                    MM(P, self.ps[ab][:, s4 * 128:(s4 + 1) * 128], kown[s][:, i, s4 * 128:(s4 + 1) * 128],
                       qt[qs][:, s4 * 128:(s4 + 1) * 128], True, True, [b_ko[s], b_qt[qs]], [self.b_ps[ab]])
                TT(P, "dve", attm[qs].rearrange("p a b -> p (a b)"), self.ps[ab][:, :], dm4[s].rearrange("p a b -> p (a b)"),
                   ALU.mult, [self.b_ps[ab], b_dm[s]], [b_am[qs]])
                rbk = 5
                for s4 in range(4):
                    MM(P, self.ps[rbk][:, s4 * 128:(s4 + 1) * 128], vown[s][:, i * 4 + s4, :], attm[qs][:, s4, :],
                       True, False, [b_vo[s], b_am[qs]], [self.b_ps[rbk]])
                    MM(P, self.ps[rbk][:, s4 * 128:(s4 + 1) * 128], Sown[:, i * 4 + s4, :],
                       qd[qs][:, s4 * 128:(s4 + 1) * 128], False, True, [b_So, b_qd[qs]], [self.b_ps[rbk]])
                rps = self.ps[rbk][:, :]
                CP(P, "act", rb, rps, [self.b_ps[rbk]], [b_rb])
                ACTV(P, rsq, rps, AF.Square, [self.b_ps[rbk]], [b_rsq])
                MM(P, self.ps[6][:, :], self.o128[:], rb, True, True, [b_rb, self.b_const], [self.b_ps[6]])
                MM(P, self.ps[7][:, :], self.o128[:], rsq, True, True, [b_rsq, self.b_const], [self.b_ps[7]])
                CP(P, "act", mean, self.ps[6][:, :], [self.b_ps[6]], [b_mean])
                TT(P, "pool", msq, mean, mean, ALU.mult, [b_mean], [b_msq])
                TT(P, "dve", var, self.ps[7][:, :], msq, ALU.subtract, [self.b_ps[7], b_msq], [b_var])
                ACTV(P, var, var, AF.Ln, [b_var, self.b_vecs], [b_var], bias=self.v("eps"), scale=1.0)
                ACTV(P, var, var, AF.Exp, [b_var], [b_var], scale=-0.5)
                TT(P, "dve", cen, rps, mean, ALU.subtract, [self.b_ps[rbk], b_mean], [b_cen])
                TT(P, "dve", cen, cen, var, ALU.mult, [b_cen, b_var], [b_cen])
                ACTV(P, cen, cen, AF.Identity, [b_cen, self.b_vecs], [b_cen], bias=self.v(("gn_b", l), 1, h),
                     scale=self.v(("gn_g", l), 1, h))
                TT(P, "pool", yo[qs], cen, rgt[qs], ALU.mult, [b_cen, b_rg[qs]], [b_yo[qs]])
                P.dma("sp", self.retoT.ap()[h, i], yo[qs], reads=[b_yo[qs]], writes=[self.b_reto])
        P.barrier()

    def make_resid_ep(self, xsrc, xdst, blk_of, chunk_of, temps, save_tail):
        P, c = self.P, self.cfg
        xo, b_xo = temps

        def ep(g, tt, res):
            blk = blk_of(tt)
            for j, (psb, w) in enumerate(res):
                m = chunk_of(g, j)
                k = self._xo_rr % len(xo)
                self._xo_rr += 1
                P.dma("sp", xo[k], xsrc.ap()[blk][:, m * T:(m + 1) * T], reads=[self.b_x[blk][m]], writes=[b_xo[k]])
                TT(P, "dve", xo[k], xo[k], self.ps[psb][:, :], ALU.add, [b_xo[k], self.b_ps[psb]], [b_xo[k]])
                if save_tail:
                    CP(P, "pool", self.xtail[:, m, blk, :], xo[k][:, T - 2:T], [b_xo[k]], [self.b_xtail])
                P.dma(self.store_q, xdst.ap()[blk][:, m * T:(m + 1) * T], xo[k], reads=[b_xo[k]], writes=[self.b_x[blk][m]])
        return ep

    def exchange_tails(self, dst, b_dst):
        P, c = self.P, self.cfg
        KD, NBo = c.KD, c.NBo
        n = KD * NBo * 2
        P.dma("sp", self.tail_own.ap()[:, :], self.xtail[:, :, :, :].rearrange("p k i t -> p (k i t)"),
              reads=[self.b_xtail], writes=[self.b_tailown])
        self.pair_gather(self.tail_own.ap(), self.tail_all.ap(), [self.b_tailown], [self.b_tailall])
        c0, c1 = self.cand
        b_c = self.b_cand
        P.dma("sp", c0[:, :, :, :], self.tail_all.ap()[0:128, :].rearrange("p (k i t) -> p k i t", k=KD, t=2),
              reads=[self.b_tailall], writes=[b_c])
        MEMSET(P, "dve", c1[:, :, 0:1, :], 0.0, [b_c])
        if NBo > 1:
            g1 = self.tail_all.ap()[128:256, :].rearrange("p (k i t) -> p k i t", k=KD, t=2)
            P.dma("sp", c1[:, :, 1:NBo, :], g1[:, :, 0:NBo - 1, :], reads=[self.b_tailall], writes=[b_c])
        dv = dst.rearrange("p k (i t) -> p k i t", t=2)
        TS(P, "dve", dv, c0[:, :, :, :], self.v("pf"), None, ALU.mult, None, [b_c, self.b_vecs], [b_dst])
        STT(P, "dve", dv, c1[:, :, :, :], self.v("omp"), dv, ALU.mult, ALU.add, [b_c, b_dst, self.b_vecs], [b_dst])

    def alloc_persist(self):
        P, c = self.P, self.cfg
        self.xtail = P.sbuf("xtail", [128, c.KD, c.NBo, 2], F32); self.b_xtail = P.buf()
        self.cand = (P.sbuf("cand0", [128, c.KD, c.NBo, 2], F32), P.sbuf("cand1", [128, c.KD, c.NBo, 2], F32))
        self.b_cand = P.buf()
        self.xpredF = P.sbuf("xpredF", [128, c.KD, c.NHALO], F32); self.b_xpredF = P.buf()
        self.kmT = P.sbuf("kmT", [128, 4, c.MEM], BF16); self.vm = P.sbuf("vm", [128, c.MEM // 128, 512], BF16)
        self.b_km = P.buf(); self.b_vm = P.buf()
        self._xo_rr = 0

    def prep_mem(self, l):
        P, c = self.P, self.cfg
        KD, MEM = c.KD, c.MEM
        self.reset_arena()
        xs = self.af32([KD, MEM]); b_xs = P.buf()
        sq = self.abf([KD, MEM]); b_sq = P.buf()
        tmp = self.af32([MEM]); b_tmp = P.buf()
        mn = self.abf([KD, MEM]); b_mn = P.buf()
        wk = self.abf([KD, 1024]); b_wk = P.buf()
        P.dma("sp", xs, self.memT.ap().rearrange("p (k m) -> p k m", k=KD), writes=[b_xs])
        self.norm_tile(xs, b_xs, MEM, ("g_mem", l), mn, b_mn, sq, b_sq, tmp, b_tmp)
        Wv = self.Wb[("w_ckv", l)].ap().rearrange("(c p) n -> p c n", p=128)
        P.dma("sp", wk, Wv, reads=[self.b_W[("w_ckv", l)]], writes=[b_wk])
        for hh in range(4):
            pb = hh % 2
            for k in range(KD):
                MM(P, self.ps[pb][:, 0:MEM], wk[:, k, hh * 128:(hh + 1) * 128], mn[:, k, :], k == 0, k == KD - 1,
                   [b_wk, b_mn], [self.b_ps[pb]])
            CP(P, "act", self.kmT[:, hh, :], self.ps[pb][:, 0:MEM], [self.b_ps[pb]], [self.b_km])
        for mc in range(MEM // 128):
            pb = 2 + mc % 2
            for k in range(KD):
                MM(P, self.ps[pb][:, :], mn[:, k, mc * 128:(mc + 1) * 128], wk[:, k, 512:1024], k == 0, k == KD - 1,
                   [b_wk, b_mn], [self.b_ps[pb]])
            CP(P, "act", self.vm[:, mc, :], self.ps[pb][:, :], [self.b_ps[pb]], [self.b_vm])
        P.barrier()

    def stage_M(self, l, st, xsrc, xdst):
        P, c = self.P, self.cfg
        KD, NBM = c.KD, self.NBM
        TSM = NBM * T
        self.reset_arena()
        X = [self.abf([NH, TSM]) for _ in range(3)]; b_X = [P.buf() for _ in range(3)]
        merged = self.abf([KD, TSM]); b_mg = P.buf()
        wbf_off = self.ab(max(2 * NH * 384 * 2, 2 * KD * 256 * 2))
        NG_ = 6
        gt = [self.abf([T]) for _ in range(NG_)]; b_gt = [P.buf() for _ in range(NG_)]
        t1 = [self.af32([T]) for _ in range(2)]; b_t1 = [P.buf(), P.buf()]
        t2 = [self.af32([T]) for _ in range(2)]; b_t2 = [P.buf(), P.buf()]
        xo = [self.af32([T]) for _ in range(3)]; b_xo = [P.buf() for _ in range(3)]
        srcs = [(self.attnT, self.b_attn), (self.convT, self.b_conv), (self.retoT, self.b_reto)]
        for br, (src, bs) in enumerate(srcs):
            for bi in range(NBM):
                blk = st * NBM + bi
                P.dma("sp", X[br][:, :, bi * T:(bi + 1) * T], src.ap()[:, blk].rearrange("h p t -> p h t"),
                      reads=[bs], writes=[b_X[br]])
        Wn = ["w_fox_o", "w_conv_o", "w_ret_o"]
        groups = [[(self.Wb[(Wn[br], l)].ap(), m * 128, 128, X[br], b_X[br]) for br in range(3)] for m in range(KD)]
        cnt = [0]

        def ep(g, tt, res):
            blk = st * NBM + tt
            k = cnt[0] % 2
            cnt[0] += 1
            gts = []
            for br in range(3):
                q = (cnt[0] * 3 + br) % NG_
                P.dma("sp", gt[q], self.gates.ap()[br * KD + g, blk], reads=[self.b_gates], writes=[b_gt[q]])
                gts.append(q)
            TT(P, "dve", t1[k], self.ps[res[0][0]][:, :], gt[gts[0]], ALU.mult, [self.b_ps[res[0][0]], b_gt[gts[0]]], [b_t1[k]])
            TT(P, "dve", t2[k], self.ps[res[1][0]][:, :], gt[gts[1]], ALU.mult, [self.b_ps[res[1][0]], b_gt[gts[1]]], [b_t2[k]])
            TT(P, "pool", t1[k], t1[k], t2[k], ALU.add, [b_t1[k], b_t2[k]], [b_t1[k]])
            TT(P, "dve", t2[k], self.ps[res[2][0]][:, :], gt[gts[2]], ALU.mult, [self.b_ps[res[2][0]], b_gt[gts[2]]], [b_t2[k]])
            TT(P, "pool", merged[:, g, tt * T:(tt + 1) * T], t1[k], t2[k], ALU.add, [b_t1[k], b_t2[k]], [b_mg])
        self.gemm_fm(NH, NBM, groups, ep, wbf_off, [0, 1, 2, 3, 4, 5], wreads=[self.b_W[(w_, l)] for w_ in Wn])
        Wo = self.Wb[("w_out", l)].ap()
        groups = [[(Wo, (2 * g + j) * 128, 128, merged, b_mg) for j in range(min(2, KD - 2 * g))] for g in range((KD + 1) // 2)]
        ep2 = self.make_resid_ep(xsrc, xdst, lambda tt: st * NBM + tt, lambda g, j: 2 * g + j, (xo, b_xo), False)
        self.gemm_fm(KD, NBM, groups, ep2, wbf_off, [0, 1, 2, 3, 4, 5], wreads=[self.b_W[("w_out", l)]])
        P.barrier()

    def stage_C(self, l, st, xr):
        P, c = self.P, self.cfg
        KD, NBM, MEM = c.KD, self.NBM, c.MEM
        TSM = NBM * T
        self.reset_arena()
        xs = self.af32([KD, T]); b_xs = P.buf()
        sq = self.abf([KD, T]); b_sq = P.buf()
        tmp = self.af32([T]); b_tmp = P.buf()
        hc = self.abf([KD, TSM]); b_hc = P.buf()
        qc = self.abf([4, TSM]); b_qc = P.buf()
        co = self.abf([4, TSM]); b_co = P.buf()
        wbf_off = self.ab(2 * KD * 256 * 2)
        pt = [self.abf([T]) for _ in range(3)]; b_pt = [P.buf() for _ in range(3)]
        rec = self.af32([T]); b_rec = P.buf()
        xo = [self.af32([T]) for _ in range(3)]; b_xo = [P.buf() for _ in range(3)]
        for bi in range(NBM):
            blk = st * NBM + bi
            P.dma("sp", xs, xr.ap()[blk].rearrange("p (k t) -> p k t", k=KD), reads=self.b_x[blk], writes=[b_xs])
            self.norm_tile(xs, b_xs, T, ("g_cross", l), hc[:, :, bi * T:(bi + 1) * T], b_hc, sq, b_sq, tmp, b_tmp)
        Wq = self.Wb[("w_cq", l)].ap()
        scale = HD ** -0.5

        def ep_q(g, tt, res):
            for j, (psb, w) in enumerate(res):
                ACTV(P, qc[:, 2 * g + j, tt * T:(tt + 1) * T], self.ps[psb][:, :], AF.Copy, [self.b_ps[psb]], [b_qc], scale=scale)
        self.gemm_fm(KD, NBM, [[(Wq, (2 * g + j) * 128, 128, hc, b_hc) for j in range(2)] for g in range(2)], ep_q,
                     wbf_off, [0, 1, 2], wreads=[self.b_W[("w_cq", l)]])
        nmc = MEM // 128
        NPT = 4
        ptc = [self.abf([T]) for _ in range(NPT)]; b_ptc = [P.buf() for _ in range(NPT)]
        itc = 0
        jj = 0
        for hh in range(4):
            for tt in range(NBM):
                ob, lb = 3 + jj % 2, 5 + jj % 2
                jj += 1
                pqs = []
                for mc in range(nmc):
                    sb = itc % 3
                    pq = itc % NPT
                    itc += 1
                    MM(P, self.ps[sb][:, :], self.kmT[:, hh, mc * 128:(mc + 1) * 128], qc[:, hh, tt * T:(tt + 1) * T],
                       True, True, [self.b_km, b_qc], [self.b_ps[sb]])
                    ACTV(P, ptc[pq], self.ps[sb][:, :], AF.Exp, [self.b_ps[sb]], [b_ptc[pq]])
                    pqs.append(pq)
                for mc, pq in enumerate(pqs):
                    MM(P, self.ps[ob][:, :], self.vm[:, mc, hh * 128:(hh + 1) * 128], ptc[pq], mc == 0, mc == nmc - 1,
                       [self.b_vm, b_ptc[pq]], [self.b_ps[ob]])
                    MM(P, self.ps[lb][:, :], self.ones[:], ptc[pq], mc == 0, mc == nmc - 1, [b_ptc[pq], self.b_const],
                       [self.b_ps[lb]])
                P.op("dve", (lambda o_, i_: (lambda e: e.reciprocal(o_, i_)))(rec, self.ps[lb][:, :]), [self.b_ps[lb]], [b_rec])
                TT(P, "dve", co[:, hh, tt * T:(tt + 1) * T], self.ps[ob][:, :], rec, ALU.mult, [self.b_ps[ob], b_rec], [b_co])
        Wco = self.Wb[("w_co", l)].ap()
        groups = [[(Wco, (2 * g + j) * 128, 128, co, b_co) for j in range(min(2, KD - 2 * g))] for g in range((KD + 1) // 2)]
        ep2 = self.make_resid_ep(xr, xr, lambda tt: st * NBM + tt, lambda g, j: 2 * g + j, (xo, b_xo), True)
        self.gemm_fm(4, NBM, groups, ep2, wbf_off, [0, 1, 2], wreads=[self.b_W[("w_co", l)]])
        P.barrier()

    def stage_F(self, l, st, xr):
        P, c = self.P, self.cfg
        KD, NBM, FC, NHALO = c.KD, self.NBM, c.FC, c.NHALO
        TSM = NBM * T
        self.reset_arena()
        hf = self.abf([KD, TSM]); b_hf = P.buf()
        hh_ = self.abf([KD, NHALO]); b_hh = P.buf()
        sqh = self.abf([KD, NHALO]); b_sqh = P.buf()
        tmp = self.af32([T]); b_tmp = P.buf()
        wbf_off = self.ab(max(2 * KD * 256 * 2, 2 * FC * 128 * 2))
        NT_ = 4
        tu = [self.af32([T + 2]) for _ in range(NT_)]; b_tu = [P.buf() for _ in range(NT_)]
        ty = [self.af32([T]) for _ in range(NT_)]; b_ty = [P.buf() for _ in range(NT_)]
        uh = [self.af32([2, NHALO]) for _ in range(2)]; b_uh = [P.buf(), P.buf()]
        xo = [self.af32([T]) for _ in range(3)]; b_xo = [P.buf() for _ in range(3)]
        act_off = self.ab(max(FC * TSM * 2, KD * T * 4 + KD * T * 2))
        act = self.av(act_off, [FC, TSM]); b_act = P.buf()
        xs = self.fv(act_off, [KD, T]); b_xs = P.buf()
        sq = self.av(act_off + KD * T * 4, [KD, T]); b_sq = P.buf()
        self.norm_tile(self.xpredF[:, :, :], self.b_xpredF, NHALO, ("g_ffn", l), hh_, b_hh, sqh, b_sqh, tmp[:, 0:NHALO], b_tmp)
        for bi in range(NBM):
            blk = st * NBM + bi
            P.dma("sp", xs, xr.ap()[blk].rearrange("p (k t) -> p k t", k=KD), reads=self.b_x[blk], writes=[b_xs])
            self.norm_tile(xs, b_xs, T, ("g_ffn", l), hf[:, :, bi * T:(bi + 1) * T], b_hf, sq, b_sq, tmp, b_tmp)
        P.barrier()
        Wu = self.Wb[("w_up", l)].ap()
        groups = [[(Wu, fa * 128, 128, hf, b_hf), (Wu, c.DFF + fa * 128, 128, hf, b_hf)] for fa in range(FC)]
        rr = [0, 0]
        cur_uh = [None]

        def ep_halo(g, hb):
            k = rr[1] % 2
            rr[1] += 1
            CP(P, "act", uh[k][:, 0, :], hb[0], [self.b_ps[7]], [b_uh[k]])
            CP(P, "act", uh[k][:, 1, :], hb[1], [self.b_ps[7]], [b_uh[k]])
            cur_uh[0] = k

        def fcw(k3, ch):
            return self.v(("fcw", l), 1, k3 * 2 * FC + ch)

        def ep(g, tt, res):
            blk = st * NBM + tt
            ku = cur_uh[0]
            ys = []
            for half, (psb, w) in enumerate(res):
                ch = g if half == 0 else FC + g
                k = rr[0] % NT_
                rr[0] += 1
                CP(P, "act", tu[k][:, 2:T + 2], self.ps[psb][:, :], [self.b_ps[psb]], [b_tu[k]])
                CP(P, "pool", tu[k][:, 0:2], uh[ku][:, half, 2 * blk:2 * blk + 2], [b_uh[ku]], [b_tu[k]])
                ACTV(P, ty[k], tu[k][:, 2:T + 2], AF.Identity, [b_tu[k], self.b_vecs], [b_ty[k]],
                     bias=self.v(("fcb", l), 1, ch), scale=fcw(2, ch))
                STT(P, "dve", ty[k], tu[k][:, 1:T + 1], fcw(1, ch), ty[k], ALU.mult, ALU.add, [b_tu[k], b_ty[k], self.b_vecs], [b_ty[k]])
                STT(P, "dve", ty[k], tu[k][:, 0:T], fcw(0, ch), ty[k], ALU.mult, ALU.add, [b_tu[k], b_ty[k], self.b_vecs], [b_ty[k]])
                ys.append(k)
            ka, kg = ys
            ACTV(P, ty[kg], ty[kg], AF.Silu, [b_ty[kg]], [b_ty[kg]])
            TT(P, "pool", act[:, g, tt * T:(tt + 1) * T], ty[ka], ty[kg], ALU.mult, [b_ty[ka], b_ty[kg]], [b_act])
        self.gemm_fm(KD, NBM, groups, ep, wbf_off, [0, 1, 2, 3, 4, 5], halo=hh_, b_halo=b_hh, nh=NHALO, ep_halo=ep_halo, wreads=[self.b_W[("w_up", l)]])
        Wd = self.Wb[("w_down", l)].ap()
        groups = [[(Wd, m * 128, 128, act, b_act)] for m in range(KD)]
        ep2 = self.make_resid_ep(xr, xr, lambda tt: st * NBM + tt, lambda g, j: g, (xo, b_xo), True)
        self.gemm_fm(FC, NBM, groups, ep2, wbf_off, [0, 1, 2, 3, 4, 5], wreads=[self.b_W[("w_down", l)]])
        P.barrier()

    def final_norm(self, xr):
        P, c = self.P, self.cfg
        KD, NBo = c.KD, c.NBo
        self.reset_arena()
        xs = [self.af32([KD, T]) for _ in range(2)]; b_xs = [P.buf(), P.buf()]
        yo = [self.af32([KD, T]) for _ in range(2)]; b_yo = [P.buf(), P.buf()]
        sq = self.abf([KD, T]); b_sq = P.buf()
        tmp = self.af32([T]); b_tmp = P.buf()
        for blk in range(NBo):
            s = blk % 2
            P.dma("sp", xs[s], xr.ap()[blk].rearrange("p (k t) -> p k t", k=KD), reads=self.b_x[blk], writes=[b_xs[s]])
            self.norm_tile(xs[s], b_xs[s], T, "g_final", yo[s], b_yo[s], sq, b_sq, tmp, b_tmp)
            P.dma("pool", xr.ap()[blk].rearrange("p (k t) -> p k t", k=KD), yo[s], reads=[b_yo[s]], writes=self.b_x[blk])

    def build(self, sharded, upto="Z"):
        P, c = self.P, self.cfg
        self.NBM = min(2, c.NBo)
        self.alloc_persist()
        self.prep_weights(sharded)
        SPLIT = False
        self.gather_weights(0, ["w_in"] if SPLIT else None)
        P.pool_alt = "dve"
        self.store_q = "sp"
        xsrc = self.xT
        xr = self.out
        xh_holder = {}
        for l in range(c.L):
            if l == 0:
                def loader(dst, buf):
                    P.dma("sp", dst, self.xpred0.ap().rearrange("p (k n) -> p k n", k=c.KD), writes=[buf])
            else:
                def loader(dst, buf):
                    self.exchange_tails(dst, buf)
            self.stage_P(l, xsrc, loader)
            if l == 0 and SPLIT:
                self.gather_weights(0, [n_ for n_ in self.weight_shapes() if n_ != "w_in"])
            if l + 1 < c.L:
                self.gather_weights(l + 1)
            if upto == "P": return
            self.compute_c(l)
            self.stage_A(l)
            if upto == "A": return
            self.stage_R(l)
            if upto == "R": return
            self.prep_mem(l)
            nst = c.NBo // self.NBM
            for st in range(nst):
                self.stage_M(l, st, xsrc, xr)
                if upto == "M": return
                self.stage_C(l, st, xr)
            if upto == "C": return
            self.exchange_tails(self.xpredF[:, :, :], self.b_xpredF)
            P.barrier()
            for st in range(nst):
                self.stage_F(l, st, xr)
            if upto == "F" and l == 0: return
            xsrc = xr
            P.pool_alt = None
            self.store_q = "pool"
        self.final_norm(xr)


_WNAMES = ["w_in", "w_fox_o", "w_conv_o", "w_ret_o", "w_out", "w_cq", "w_ckv", "w_co", "w_up", "w_down"]


def _core_inputs(cfg, inp, c, sharded=True):
    b, p = c // 2, c % 2
    ct = const_tables(cfg, p)
    xb = np.asarray(inp["x"][b], np.float32)
    d = dict(xT=x_to_blocks(cfg, xb, p), xpred0=xpred_host(cfg, xb, p),
             memT=np.ascontiguousarray(np.asarray(inp["mem"][b], np.float32).T.reshape(cfg.KD, 128, cfg.MEM)
                                       .transpose(1, 0, 2).reshape(128, -1)),
             vecs=build_vecs(cfg, inp, p), rope=ct["rope"], decq=ct["decq"], dmaskT=ct["dmaskT"], dkT=ct["dkT"],
             masks=ct["masks"], pm=ct["pm"], ident=ct["ident"])
    for nm in _WNAMES:
        w = np.asarray(inp[nm], dtype=np.float32)
        if sharded:
            rs = w.shape[1] // 8
            w = w[:, c * rs:(c + 1) * rs, :]
        d[nm] = np.ascontiguousarray(w)
    return d


def kernel_impl(cfg, inputs, sharded=True):
    inp = {k: np.asarray(v) for k, v in inputs.items()}
    m = MK(cfg)
    m.build(sharded=sharded)
    m.P.emit()
    in_maps = [_core_inputs(cfg, inp, c, sharded) for c in range(8)]
    res = run_bass_kernel_spmd(m.P.nc, in_maps, core_ids=list(range(8)))
    out = np.zeros((cfg.B, cfg.S, cfg.D), np.float32)
    for c in range(8):
        blocks_to_x(cfg, np.asarray(res.results[c]["out"], np.float32).reshape(cfg.NBo, 128, cfg.KD * T), c % 2,
                    out[c // 2])
    return out


def kernel(**inputs):
    return kernel_impl(Cfg(), inputs)
```

```python
import contextlib
import numpy as np
import concourse.bass as bass
import concourse.mybir as mybir

F32 = mybir.dt.float32
BF16 = mybir.dt.bfloat16
AF = mybir.ActivationFunctionType
ALU = mybir.AluOpType
AX = mybir.AxisListType

ENGS = ("pe", "act", "dve", "pool", "sp")


class Buf:
    __slots__ = ("name", "last_write", "readers")

    def __init__(self, name):
        self.name = name
        self.last_write = None
        self.readers = []


class Op:
    __slots__ = ("eng", "fn", "deps", "is_dma", "signal", "count", "sem", "semval", "idx", "prewait", "inc")

    def __init__(self, eng, fn, is_dma):
        self.eng = eng
        self.fn = fn
        self.deps = []
        self.is_dma = is_dma
        self.signal = False
        self.count = None
        self.sem = None
        self.semval = None
        self.prewait = None


class Prog:
    N_DMA_SEMS = 56
    SEM_CLASSES = {"sp": (0, 20), "act": (20, 12), "pool": (32, 12), "cc": (44, 12)}

    def __init__(self):
        self.nc = bass.Bass("TRN2", target_bir_lowering=False)
        self.stack = contextlib.ExitStack()
        self.ops = {e: [] for e in ENGS}
        self.dma_rr = {k: 0 for k in self.SEM_CLASSES}
        self.dma_sem_use = [0] * self.N_DMA_SEMS
        self.dma_sem_last = [None] * self.N_DMA_SEMS
        self.dma_sem_bar = [0] * self.N_DMA_SEMS
        self.bg = False
        self.nbuf = 0
        self.sb_bytes = 0

    def sbuf(self, name, shape, dt):
        t = self.stack.enter_context(self.nc.sbuf_tensor(name, list(shape), dt))
        n = 1
        for s in shape[1:]:
            n *= s
        self.sb_bytes += n * mybir.dt.size(dt)
        return t

    def psum(self, name, shape, dt=F32):
        return self.stack.enter_context(self.nc.psum_tensor(name, list(shape), dt))

    def dram(self, name, shape, dt, kind="Internal"):
        if kind == "Internal":
            return self.nc.dram_tensor(name, list(shape), dt)
        return self.nc.dram_tensor(name, list(shape), dt, kind=kind)

    def buf(self, name=None):
        self.nbuf += 1
        return Buf(name or f"b{self.nbuf}")

    def _record(self, eng, fn, reads, writes, is_dma=False, inc=16):
        op = Op(eng, fn, is_dma)
        op.inc = inc
        deps = []
        for r in reads:
            if r.last_write is not None:
                deps.append(r.last_write)
        for w in writes:
            if w.last_write is not None:
                deps.append(w.last_write)
            deps.extend(w.readers)
        seen = set()
        for d in deps:
            if d is op or id(d) in seen:
                continue
            seen.add(id(d))
            if d.eng == "pe" and eng == "pe" and not d.is_dma:
                continue
            op.deps.append(d)
            if not d.is_dma:
                d.signal = True
        for w in writes:
            w.last_write = op
            w.readers = []
        for r in reads:
            r.readers.append(op)
        if is_dma:
            cls = "cc" if inc == 1 else eng
            base, cnt_ = self.SEM_CLASSES[cls]
            s = base + self.dma_rr[cls] % cnt_
            self.dma_rr[cls] += 1
            op.prewait = self.dma_sem_last[s]
            self.dma_sem_use[s] += inc
            op.sem = s
            op.semval = self.dma_sem_use[s]
            self.dma_sem_last[s] = op
            if cls != "cc" and not self.bg:
                self.dma_sem_bar[s] = op.semval
        self.ops[eng].append(op)
        return op

    pool_alt = None

    def op(self, eng, fn, reads=(), writes=()):
        if eng == "pool" and self.pool_alt:
            eng = self.pool_alt
        return self._record(eng, fn, reads, writes, False)

    def dma(self, eng, out, in_, reads=(), writes=(), **kw):
        return self._record(eng, lambda e: e.dma_start(out=out, in_=in_, **kw), reads, writes, True)

    def custom_dma(self, eng, fn, reads=(), writes=()):
        return self._record(eng, fn, reads, writes, True)

    def emit(self, final_waits=()):
        nc = self.nc
        esem = {e: self.stack.enter_context(nc.semaphore(f"s_{e}")) for e in ENGS}
        dsem = [self.stack.enter_context(nc.semaphore(f"d_{i}")) for i in range(self.N_DMA_SEMS)]
        for e in ENGS:
            c = 0
            for op in self.ops[e]:
                if op.signal and not op.is_dma:
                    c += 1
                    op.count = c
        block = self.stack.enter_context(nc.Block())
        ninst = 0

        def run(e, eng):
            waited_e = {x: 0 for x in ENGS}
            waited_d = [0] * self.N_DMA_SEMS
            n = 0
            for op in self.ops[e]:
                if op.fn is None:
                    for (s_, v_) in op.prewait:
                        if waited_d[s_] < v_:
                            eng.wait_ge(dsem[s_], v_)
                            waited_d[s_] = v_
                            n += 1
                    for d in op.deps:
                        if d.eng == e:
                            continue
                        if waited_e[d.eng] < d.count:
                            eng.wait_ge(esem[d.eng], d.count)
                            waited_e[d.eng] = d.count
                            n += 1
                    continue
                if op.is_dma and op.prewait is not None:
                    p = op.prewait
                    if waited_d[p.sem] < p.semval:
                        eng.wait_ge(dsem[p.sem], p.semval)
                        waited_d[p.sem] = p.semval
                        n += 1
                need_e = {}
                need_d = {}
                for d in op.deps:
                    if d.is_dma:
                        if need_d.get(d.sem, 0) < d.semval:
                            need_d[d.sem] = d.semval
                    else:
                        if need_e.get(d.eng, 0) < d.count:
                            need_e[d.eng] = d.count
                for s_, v_ in need_d.items():
                    if waited_d[s_] < v_:
                        eng.wait_ge(dsem[s_], v_)
                        waited_d[s_] = v_
                        n += 1
                for e_, v_ in need_e.items():
                    if waited_e[e_] < v_:
                        eng.wait_ge(esem[e_], v_)
                        waited_e[e_] = v_
                        n += 1
                ins = op.fn(eng)
                n += 1
                if op.is_dma:
                    ins.then_inc(dsem[op.sem], op.inc)
                elif op.signal:
                    ins.then_inc(esem[e], 1)
            last = {}
            for op in self.ops[e]:
                if op.is_dma:
                    last[op.sem] = op.semval
            for s, v in last.items():
                if waited_d[s] < v:
                    eng.wait_ge(dsem[s], v)
            return n

        counts = {}

        @block.tensor
        def _(t):
            counts["pe"] = run("pe", t)

        @block.scalar
        def _(s):
            counts["act"] = run("act", s)

        @block.vector
        def _(v):
            counts["dve"] = run("dve", v)

        @block.gpsimd
        def _(g):
            counts["pool"] = run("pool", g)

        @block.sync
        def _(sp):
            counts["sp"] = run("sp", sp)

        self.stack.close()
        return counts


def _barrier(self):
    lasts = []
    for e in ENGS:
        for op in reversed(self.ops[e]):
            if not op.is_dma and op.fn is not None:
                op.signal = True
                lasts.append(op)
                break
    snap = [(s, v) for s, v in enumerate(self.dma_sem_bar) if v > 0]
    for e in ENGS:
        b = Op(e, None, False)
        b.deps = list(lasts)
        b.prewait = snap
        self.ops[e].append(b)


Prog.barrier = _barrier


def MM(P, out, lhsT, rhs, start, stop, reads, writes, **kw):
    return P.op("pe", lambda e: e.matmul(out, lhsT, rhs, start=start, stop=stop, **kw), reads, writes)


def TR(P, out, in_, ident, reads, writes):
    return P.op("pe", lambda e: e.transpose(out, in_, ident), reads, writes)


def ACTV(P, out, in_, func, reads, writes, bias=None, scale=None, accum_out=None, eng="act"):
    kw = {}
    if bias is not None:
        kw["bias"] = bias
    if scale is not None:
        kw["scale"] = scale
    if accum_out is not None:
        kw["accum_out"] = accum_out
    return P.op(eng, lambda e: e.activation(out, in_, func, **kw), reads, writes)


def TS(P, eng, out, in0, s1, s2, op0, op1, reads, writes):
    if op1 is None:
        return P.op(eng, lambda e: e.tensor_scalar(out, in0, s1, s2, op0), reads, writes)
    return P.op(eng, lambda e: e.tensor_scalar(out, in0, s1, s2, op0, op1), reads, writes)


def STT(P, eng, out, in0, scalar, in1, op0, op1, reads, writes):
    return P.op(eng, lambda e: e.scalar_tensor_tensor(out, in0, scalar, in1, op0, op1), reads, writes)


def TT(P, eng, out, in0, in1, op, reads, writes):
    return P.op(eng, lambda e: e.tensor_tensor(out, in0, in1, op), reads, writes)


def CP(P, eng, out, in_, reads, writes):
    if eng == "act":
        return P.op(eng, lambda e: e.copy(out, in_), reads, writes)
    return P.op(eng, lambda e: e.tensor_copy(out, in_), reads, writes)


def MEMSET(P, eng, ap, val, writes):
    return P.op(eng, lambda e: e.memset(ap, val), (), writes)


import numpy as np
import ml_dtypes
from concourse.bass_utils import run_bass_kernel_spmd

T = 512
HD = 128
NH = 8
HW = NH * HD
EPS = 1e-6
NEG = -30000.0


class Cfg:
    def __init__(self, B=4, S=8192, D=2048, DFF=5632, L=2, MEM=256):
        self.B, self.S, self.D, self.DFF, self.L, self.MEM = B, S, D, DFF, L, MEM
        self.KD = D // 128
        self.NBo = S // (2 * T)
        self.TO = self.NBo * T
        self.FC = DFF // 128
        self.NIN = 3 * HW + NH + 3 * HW + 4 * HW + 3 * D
        o = {}
        off = 0
        for nm, sz in [("fq", HW), ("fk", HW), ("fv", HW), ("fl", NH), ("cb", HW), ("cc", HW), ("ch", HW),
                       ("rq", HW), ("rk", HW), ("rv", HW), ("rg", HW), ("ga", D), ("gb", D), ("gc", D)]:
            o[nm] = off
            off += sz
        self.o = o
        self.NBS = min(4, self.NBo)
        self.NCH = S // 128
        self.NHALO = self.NBo * 2


def vec_layout(cfg):
    lay = {}
    off = 0

    def add(nm, ncol):
        nonlocal off
        lay[nm] = off
        off += ncol

    for l in range(cfg.L):
        add(("g_mix", l), cfg.KD)
        add(("b_gate", l), 3 * cfg.KD)
        add(("conv_w", l), 3 * NH)
        add(("gn_g", l), NH)
        add(("gn_b", l), NH)
        add(("b_f", l), 1)
        add(("nb_f", l), 1)
        add(("g_cross", l), cfg.KD)
        add(("g_mem", l), cfg.KD)
        add(("g_ffn", l), cfg.KD)
        add(("fcw", l), 3 * 2 * cfg.FC)
        add(("fcb", l), 2 * cfg.FC)
    add("g_final", cfg.KD)
    add("pf", 1)
    add("omp", 1)
    add("eps", 1)
    add("one", 1)
    return lay, off


def colmajor(v):
    v = np.asarray(v, np.float32)
    return np.ascontiguousarray(v.reshape(-1, 128).T)


def build_vecs(cfg, inp, p):
    lay, nv = vec_layout(cfg)
    V = np.zeros((128, nv), np.float32)

    def put(key, arr2d):
        V[:, lay[key]:lay[key] + arr2d.shape[1]] = arr2d

    for l in range(cfg.L):
        put(("g_mix", l), colmajor(inp["g_mix"][l]))
        put(("b_gate", l), colmajor(inp["b_gate"][l]))
        cw = np.asarray(inp["conv_w"][l], np.float32)
        put(("conv_w", l), np.concatenate([colmajor(cw[k]) for k in range(3)], axis=1))
        put(("gn_g", l), colmajor(inp["ret_gn_g"][l]))
        put(("gn_b", l), colmajor(inp["ret_gn_b"][l]))
        bf = np.zeros((128, 1), np.float32)
        bf[:NH, 0] = np.asarray(inp["b_f"][l], np.float32)
        put(("b_f", l), bf)
        put(("nb_f", l), bf)
        put(("g_cross", l), colmajor(inp["g_cross"][l]))
        put(("g_mem", l), colmajor(inp["g_mem"][l]))
        put(("g_ffn", l), colmajor(inp["g_ffn"][l]))
        fw_ = np.asarray(inp["ffn_conv_w"][l], np.float32)
        put(("fcw", l), np.concatenate([colmajor(fw_[k]) for k in range(3)], axis=1))
        put(("fcb", l), colmajor(inp["ffn_conv_b"][l]))
    put("g_final", colmajor(inp["g_final"]))
    V[:, lay["pf"]] = float(p)
    V[:, lay["omp"]] = 1.0 - float(p)
    V[:, lay["eps"]] = EPS
    V[:, lay["one"]] = 1.0
    return V


def const_tables(cfg, p):
    S, NBo = cfg.S, cfg.NBo
    half = HD // 2
    inv = 10000.0 ** (-np.arange(half, dtype=np.float32) / half)
    pos = np.concatenate([np.arange((2 * i + p) * T, (2 * i + p + 1) * T) for i in range(NBo)]).astype(np.float32)
    ang = pos[:, None] * inv[None, :]
    cos = np.cos(ang).astype(np.float32).T
    sin = np.sin(ang).astype(np.float32).T
    cosT = np.concatenate([cos, cos], 0)
    sinT = np.concatenate([sin, sin], 0)
    rope = np.stack([cosT, sinT], 0)
    gam = 1.0 - 2.0 ** (-5.0 - np.arange(NH, dtype=np.float32))
    lg = np.log(gam.astype(np.float32)).astype(np.float32)
    idx = np.arange(128, dtype=np.float32)
    scale = HD ** -0.5
    dq = np.exp(lg[:, None] * (idx[None, :] + 1.0)).astype(np.float32) * scale
    decq = np.broadcast_to(np.tile(dq, (1, T // 128))[:, None, :], (NH, 128, T)).astype(np.float32)
    diff = idx[None, :] - idx[:, None]
    dm = np.where(diff >= 0, np.exp(lg[:, None, None] * np.maximum(diff, 0.0)[None]), 0.0).astype(np.float32)
    dk = np.exp(lg[:, None] * (127.0 - idx)[None, :]).astype(np.float32)
    dkT = np.ascontiguousarray(dk.T)
    cdec = np.exp(lg * 128.0).astype(np.float32)
    tk = np.arange(128)[:, None]
    tq = np.arange(T)[None, :]
    masks = np.zeros((8, 128, T), np.float32)
    for r in range(8):
        if p == 0:
            if r < 4:
                masks[r] = np.where(r * 128 + tk <= tq, 0.0, NEG)
            else:
                masks[r] = NEG
        else:
            if r < 4:
                masks[r] = 0.0
            else:
                masks[r] = np.where((r - 4) * 128 + tk <= tq, 0.0, NEG)
    pm = np.zeros((128, 128), np.float32)
    for m in range(128):
        if m < 64:
            pm[m + 64, m] = -1.0
        else:
            pm[m - 64, m] = 1.0
    return dict(rope=rope, decq=decq, dmaskT=dm, dkT=dkT, cdec=cdec,
                masks=masks.astype(ml_dtypes.bfloat16), pm=pm,
                ident=np.eye(128, dtype=np.float32))


def _ldiv(n, lim):
    d = max(1, min(n, lim))
    while n % d:
        d -= 1
    return d


def gather_plan(R, C):
    nr1 = _ldiv(R // 8, (1024 * 1024) // (2 * C))
    nr2 = _ldiv(R // 2, (2 * 1024 * 1024) // (2 * C))
    return nr1, nr2


def shard_rows(R, C, c):
    nr1, nr2 = gather_plan(R, C)
    g, r = c // 4, c % 4
    s = np.arange(R // 8)
    h = (s // nr1) * 4 * nr1 + r * nr1 + (s % nr1)
    return (h // nr2) * 2 * nr2 + g * nr2 + (h % nr2)


def x_to_blocks(cfg, xb, p):
    out = np.empty((cfg.NBo, 128, cfg.KD * T), np.float32)
    for i in range(cfg.NBo):
        J = 2 * i + p
        blk = xb[J * T:(J + 1) * T, :]
        out[i] = blk.T.reshape(cfg.KD, 128, T).transpose(1, 0, 2).reshape(128, cfg.KD * T)
    return out


def blocks_to_x(cfg, blocks, p, xb_out):
    for i in range(cfg.NBo):
        J = 2 * i + p
        blk = blocks[i].reshape(128, cfg.KD, T).transpose(1, 0, 2).reshape(cfg.D, T)
        xb_out[J * T:(J + 1) * T, :] = blk.T


def xpred_host(cfg, xb, p):
    out = np.zeros((128, cfg.KD, cfg.NBo, 2), np.float32)
    for i in range(cfg.NBo):
        J = 2 * i + p
        if J == 0:
            continue
        tail = xb[J * T - 2:J * T, :]
        out[:, :, i, :] = tail.T.reshape(cfg.KD, 128, 2).transpose(1, 0, 2)
    return out.reshape(128, -1)


class MK:
    def __init__(self, cfg, taps=()):
        self.cfg = cfg
        self.taps = set(taps)
        self.P = Prog()
        P = self.P
        cfg_ = cfg
        KD, NBo, TO = cfg.KD, cfg.NBo, cfg.TO
        self.lay, self.NV = vec_layout(cfg)
        self.xT = P.dram("xT", [NBo, 128, KD * T], F32, kind="ExternalInput")
        self.out = P.dram("out", [NBo, 128, KD * T], F32, kind="ExternalOutput")
        self.xpred0 = P.dram("xpred0", [128, KD * NBo * 2], F32, kind="ExternalInput")
        self.memT = P.dram("memT", [128, KD * cfg.MEM], F32, kind="ExternalInput")
        self.vecs_d = P.dram("vecs", [128, self.NV], F32, kind="ExternalInput")
        self.rope_d = P.dram("rope", [2, 128, TO], F32, kind="ExternalInput")
        self.decq_d = P.dram("decq", [NH, 128, T], F32, kind="ExternalInput")
        self.dmaskT_d = P.dram("dmaskT", [NH, 128, 128], F32, kind="ExternalInput")
        self.dkT_d = P.dram("dkT", [128, NH], F32, kind="ExternalInput")
        self.masks_d = P.dram("masks", [8, 128, T], BF16, kind="ExternalInput")
        self.pm_d = P.dram("pm", [128, 128], F32, kind="ExternalInput")
        self.ident_d = P.dram("ident", [128, 128], F32, kind="ExternalInput")
        self.W = {}
        self.vecs = P.sbuf("vecs_sb", [128, self.NV], F32)
        self.b_vecs = P.buf("vecs")
        self.onesD = P.sbuf("onesD", [128, 128], BF16)
        self.ones = P.sbuf("ones", [128, 128], BF16)
        self.ones_f = P.sbuf("ones_f", [128, 128], F32)
        self.o128 = P.sbuf("o128", [128, 128], BF16)
        self.ident_f = P.sbuf("ident_f", [128, 128], F32)
        self.ident_b = P.sbuf("ident_b", [128, 128], BF16)
        self.pm_f = P.sbuf("pm_f", [128, 128], F32)
        self.masks = P.sbuf("masks_sb", [128, 8, T], BF16)
        self.b_const = P.buf("const")
        P.dma("sp", self.vecs[:], self.vecs_d[:, :], writes=[self.b_vecs])
        P.dma("sp", self.ident_f[:], self.ident_d[:, :], writes=[self.b_const])
        P.dma("sp", self.pm_f[:], self.pm_d[:, :], writes=[self.b_const])
        P.dma("sp", self.masks[:], self.masks_d.ap().rearrange("r p t -> p r t"), writes=[self.b_const])
        MEMSET(P, "dve", self.onesD[:], 1.0 / cfg.D, [self.b_const])
        MEMSET(P, "dve", self.ones[:], 1.0, [self.b_const])
        MEMSET(P, "dve", self.ones_f[:], 1.0, [self.b_const])
        MEMSET(P, "dve", self.o128[:], 1.0 / 128.0, [self.b_const])
        CP(P, "dve", self.ident_b[:], self.ident_f[:], [self.b_const], [self.b_const])
        for l in range(cfg.L):
            c = self.lay[("nb_f", l)]
            TS(P, "dve", self.vecs[:, c:c + 1], self.vecs[:, c:c + 1], -1.0, None, ALU.mult, None,
               [self.b_vecs], [self.b_vecs])
        self.ps = [P.psum(f"ps{i}", [128, T], F32) for i in range(8)]
        self.b_ps = [P.buf(f"ps{i}") for i in range(8)]
        self.ABYTES = 176 * 1024
        self.arena = P.sbuf("arena", [128, self.ABYTES // 2], BF16)
        self.abump = 0
        self.alloc_scratch()

    def weight_shapes(self):
        c = self.cfg
        return dict(w_in=(c.D, c.NIN), w_fox_o=(HW, c.D), w_conv_o=(HW, c.D), w_ret_o=(HW, c.D),
                    w_out=(c.D, c.D), w_cq=(c.D, 512), w_ckv=(c.D, 1024), w_co=(512, c.D),
                    w_up=(c.D, 2 * c.DFF), w_down=(c.DFF, c.D))

    def v(self, key, ncol=1, off=0):
        c = self.lay[key] + off
        return self.vecs[:, c:c + ncol]

    def dr(self, name, shape, dt):
        kind = "ExternalOutput" if name in self.taps else "Internal"
        return self.P.dram(name, shape, dt, kind=kind)

    def alloc_scratch(self):
        c = self.cfg
        P = self.P
        NBo, TO = c.NBo, c.TO
        self.qT = self.dr("qT", [NH, NBo, 128, T], BF16)
        self.rqT = self.dr("rqT", [NH, NBo, 128, T], BF16)
        self.rqdT = self.dr("rqdT", [NH, NBo, 128, T], BF16)
        self.rgT = self.dr("rgT", [NH, NBo, 128, T], BF16)
        self.convT = self.dr("convT", [NH, NBo, 128, T], BF16)
        self.gates = self.dr("gates", [3 * c.KD, NBo, 128, T], BF16)
        self.attnT = self.dr("attnT", [NH, NBo, 128, T], BF16)
        self.retoT = self.dr("retoT", [NH, NBo, 128, T], BF16)
        self.XR = NH * 128 * NBo
        self.RC = min(2048, self.XR)
        self.NCHK = 4 * self.XR // self.RC
        self.HPC = self.RC // (128 * NBo)
        self.xch_own = P.dram("xch_own", [4 * self.XR, T], BF16)
        self.xch_all = P.dram("xch_all", [self.NCHK, 2 * self.RC, T], BF16)
        self.b_xallk = [P.buf() for _ in range(self.NCHK)]
        self.ls_own = P.dram("ls_own", [NH, TO], F32)
        self.ls_all = P.dram("ls_all", [2 * NH, TO], F32)
        self.cq_d = self.dr("cq_d", [NH, TO], F32)
        self.tail_own = P.dram("tail_own", [128, c.KD * NBo * 2], F32)
        self.tail_all = P.dram("tail_all", [256, c.KD * NBo * 2], F32)
        self.b_q = P.buf(); self.b_rq = P.buf(); self.b_rg = P.buf(); self.b_conv = P.buf()
        self.b_gates = P.buf(); self.b_attn = P.buf(); self.b_reto = P.buf()
        self.b_xown = P.buf(); self.b_xall = P.buf(); self.b_lsown = P.buf(); self.b_lsall = P.buf()
        self.b_cq = P.buf(); self.b_tailown = P.buf(); self.b_tailall = P.buf()
        self.b_x = [[P.buf(f"x{i}_{k}") for k in range(c.KD)] for i in range(NBo)]

    def xch_view(self, which, region, r=None):
        XR = self.XR
        assert which == "own"
        base = self.xch_own.ap()[region * XR:(region + 1) * XR, :]
        return base.rearrange("(h p i) t -> h p i t", h=NH, p=128)

    def xall(self, region, r, h):
        NBo = self.cfg.NBo
        k = (region * self.XR) // self.RC + h // self.HPC
        o = r * self.RC + (h % self.HPC) * 128 * NBo
        ap = self.xch_all.ap()[k, o:o + 128 * NBo, :].rearrange("(p i) t -> p i t", p=128)
        return ap, self.b_xallk[k]

    def pair_gather(self, src_ap, dst_ap, reads, writes):
        groups = [[0, 1], [2, 3], [4, 5], [6, 7]]
        return self.P._record("pool", lambda e: e.collective_compute(
            "AllGather", ALU.bypass, replica_groups=groups, ins=[src_ap.opt()], outs=[dst_ap.opt()]),
            reads, writes, is_dma=True, inc=1)

    def reset_arena(self):
        self.abump = 0
        self.region_bufs = {}

    def ab(self, nbytes):
        o = self.abump
        self.abump += (nbytes + 3) // 4 * 4
        assert self.abump <= self.ABYTES, (self.abump, self.ABYTES)
        return o

    def _shape(self, v, shape):
        if len(shape) == 2:
            return v.rearrange("p (a b) -> p a b", a=shape[0])
        if len(shape) == 3:
            return v.rearrange("p (a b c) -> p a b c", a=shape[0], b=shape[1])
        return v

    def av(self, boff, shape):
        n = int(np.prod(shape))
        assert boff % 2 == 0 and boff + 2 * n <= self.ABYTES
        return self._shape(self.arena[:, boff // 2:boff // 2 + n], shape)

    def fv(self, boff, shape):
        n = int(np.prod(shape))
        assert boff % 4 == 0 and boff + 4 * n <= self.ABYTES
        return self._shape(self.arena[:, boff // 2:boff // 2 + 2 * n].bitcast(F32), shape)

    def abf(self, shape):
        n = int(np.prod(shape))
        return self.av(self.ab(2 * n), shape)

    def af32(self, shape):
        n = int(np.prod(shape))
        return self.fv(self.ab(4 * n), shape)

    def norm_tile(self, xs, b_xs, n, gkey, hT_out, b_h, sq, b_sq, tmp, b_tmp, psb=7):
        P, c = self.P, self.cfg
        KD = c.KD
        ACTV(P, sq, xs, AF.Square, [b_xs], [b_sq])
        ps = self.ps[psb][:, 0:n]
        for k in range(KD):
            MM(P, ps, self.onesD[:], sq[:, k, :], k == 0, k == KD - 1, [b_sq, self.b_const], [self.b_ps[psb]])
        ACTV(P, tmp, ps, AF.Ln, [self.b_ps[psb], self.b_vecs], [b_tmp], bias=self.v("eps"), scale=1.0)
        ACTV(P, tmp, tmp, AF.Exp, [b_tmp], [b_tmp], scale=-0.5)
        for k in range(KD):
            STT(P, "dve", hT_out[:, k, :], xs[:, k, :], self.v(gkey, 1, k), tmp, ALU.mult, ALU.mult,
                [b_xs, b_tmp, self.b_vecs], [b_h])

    def gemm_fm(self, Kc, ntt, groups, ep, wbf_off, banks, halo=None, b_halo=None, nh=0, ep_halo=None, nslot=2, wreads=()):
        P = self.P
        gmax = max(len(g) for g in groups)
        GW = gmax * 128
        wbf = [self.av(wbf_off + s * Kc * GW * 2, [Kc, GW]) for s in range(nslot)]
        b_wbf = [P.buf() for _ in range(nslot)]
        prev = self.region_bufs.get(wbf_off, [])
        self.region_bufs[wbf_off] = b_wbf

        def load(g):
            s = g % nslot
            ents = groups[g]
            j = 0
            while j < len(ents):
                W2d, co, w = ents[j][0], ents[j][1], ents[j][2]
                j2 = j + 1
                tot = w
                while (j2 < len(ents) and ents[j2][0] is W2d and ents[j2][1] == co + tot and w == 128
                       and ents[j2 - 1][2] == 128):
                    tot += ents[j2][2]
                    j2 += 1
                Wv = W2d.rearrange("(c p) n -> p c n", p=128)
                P.dma("sp", wbf[s][:, :, j * 128:j * 128 + tot], Wv[:, :, co:co + tot], reads=self._flat(wreads),
                      writes=[b_wbf[s]] + (prev if g < nslot else []))
                j = j2

        ng = len(groups)
        for g in range(min(nslot - 1, ng)):
            load(g)
        bi = 0
        for g in range(ng):
            s = g % nslot
            if g + nslot - 1 < ng:
                load(g + nslot - 1)
            if halo is not None:
                hb = []
                for j, (W2d, co, w, xT, b_x) in enumerate(groups[g]):
                    ps = self.ps[7][0:w, j * nh:(j + 1) * nh]
                    for k in range(Kc):
                        MM(P, ps, wbf[s][:, k, j * 128:j * 128 + w], halo[:, k, :], k == 0, k == Kc - 1,
                           [b_wbf[s], b_halo], [self.b_ps[7]])
                    hb.append(ps)
                ep_halo(g, hb)
            for tt in range(ntt):
                res = []
                for j, (W2d, co, w, xT, b_x) in enumerate(groups[g]):
                    psb = banks[bi % len(banks)]
                    bi += 1
                    ps = self.ps[psb][0:w, :]
                    for k in range(Kc):
                        MM(P, ps, wbf[s][:, k, j * 128:j * 128 + w], xT[:, k, tt * T:(tt + 1) * T],
                           k == 0, k == Kc - 1, [b_wbf[s], b_x], [self.b_ps[psb]])
                    res.append((psb, w))
                ep(g, tt, res)

    def gemm_tm(self, W2d, Kc, xT, b_x, ntc, col0, ncols, ep, wbf_off, banks, wreads=()):
        P = self.P
        GW = 256
        ng = ncols // GW
        wbf = [self.av(wbf_off + s * Kc * GW * 2, [Kc, GW]) for s in range(2)]
        b_wbf = [P.buf(), P.buf()]
        prev = self.region_bufs.get(wbf_off, [])
        self.region_bufs[wbf_off] = b_wbf
        Wv = W2d.rearrange("(c p) n -> p c n", p=128)

        def load(g):
            s = g % 2
            P.dma("sp", wbf[s][:, :, :], Wv[:, :, col0 + g * GW:col0 + (g + 1) * GW], reads=self._flat(wreads),
                  writes=[b_wbf[s]] + (prev if g < 2 else []))

        load(0)
        bi = 0
        for g in range(ng):
            s = g % 2
            if g + 1 < ng:
                load(g + 1)
            for tc in range(ntc):
                psb = banks[bi % len(banks)]
                bi += 1
                ps = self.ps[psb][:, 0:GW]
                for k in range(Kc):
                    MM(P, ps, xT[:, k, tc * 128:(tc + 1) * 128], wbf[s][:, k, :], k == 0, k == Kc - 1,
                       [b_wbf[s], b_x], [self.b_ps[psb]])
                ep(g, tc, psb)

    def prep_weights(self, sharded):
        P, c = self.P, self.cfg
        self.sharded = sharded
        self.Wb = {}
        self.b_W = {}
        self.shard = {}
        CH = 2048
        self.reset_arena()
        st = [self.af32([CH]) for s in range(3)]
        sb = [self.abf([CH]) for s in range(3)]
        b_st = [P.buf() for _ in range(3)]
        b_sb = [P.buf() for _ in range(3)]
        it = 0
        engs = ["dve", "act", "pool"]
        for nm, (r, cc) in self.weight_shapes().items():
            rows = r // 8 if sharded else r
            self.W[nm] = P.dram(nm, [c.L, rows, cc], F32, kind="ExternalInput")
        for l in range(c.L):
            for nm, (r, cc) in self.weight_shapes().items():
                full = P.dram(f"{nm}_bf{l}", [r, cc], BF16)
                b_full = P.buf()
                self.Wb[(nm, l)] = full
                self.b_W[(nm, l)] = b_full
                rows = r // 8 if sharded else r
                if sharded:
                    shard = P.dram(f"{nm}_sh{l}", [rows, cc], BF16)
                    b_shard = P.buf()
                    self.shard[(nm, l)] = (shard, b_shard)
                    dst_t, b_dst = shard, b_shard
                else:
                    dst_t, b_dst = full, b_full
                src = self.W[nm].ap()[l].rearrange("r c -> (r c)").rearrange("(p n) -> p n", p=128)
                dst = dst_t.ap().rearrange("r c -> (r c)").rearrange("(p n) -> p n", p=128)
                n = rows * cc // 128
                assert rows * cc % 128 == 0
                for o in range(0, n, CH):
                    w = min(CH, n - o)
                    s = it % 3
                    P.dma("sp", st[s][:, 0:w], src[:, o:o + w], writes=[b_st[s]])
                    CP(P, engs[it % 3], sb[s][:, 0:w], st[s][:, 0:w], [b_st[s]], [b_sb[s]])
                    P.dma("act", dst[:, o:o + w], sb[s][:, 0:w], reads=[b_sb[s]], writes=[b_dst])
                    it += 1
        P.barrier()

    @staticmethod
    def _flat(ws):
        out = []
        for w in ws:
            out.extend(w if isinstance(w, list) else [w])
        return out

    def gather_weights(self, l, names=None):
        if not self.sharded:
            return
        P, c = self.P, self.cfg
        G4 = [[0, 1, 2, 3], [4, 5, 6, 7]]
        G2 = [[0, 4], [1, 5], [2, 6], [3, 7]]
        LIM = 4 * 1024 * 1024
        if not hasattr(self, "_nstg"):
            self._nstg = 0

        def new_stg():
            self._nstg += 1
            return P.dram(f"stg_{self._nstg}", [LIM // 2], BF16), P.buf()

        def cc(groups, a_, b_, reads, writes):
            return P._record("pool", lambda e: e.collective_compute(
                "AllGather", ALU.bypass, replica_groups=groups, ins=[a_.opt()], outs=[b_.opt()]),
                reads, writes, is_dma=True, inc=1)

        P.bg = True
        todo = [(nm, r, cc_) for nm, (r, cc_) in self.weight_shapes().items() if names is None or nm in names]
        st1 = {}
        for nm, r, cc_ in todo:
            shard, b_shard = self.shard[(nm, l)]
            nr1, nr2 = gather_plan(r, cc_)
            half = P.dram(f"{nm}_hf{l}", [r // 2, cc_], BF16)
            bh = []
            for k in range((r // 8) // nr1):
                b = P.buf()
                cc(G4, shard.ap()[k * nr1:(k + 1) * nr1, :], half.ap()[k * 4 * nr1:(k + 1) * 4 * nr1, :], [b_shard], [b])
                bh.append(b)
            st1[nm] = (half, bh, nr1, nr2)
        for nm, r, cc_ in todo:
            half, bh, nr1, nr2 = st1[nm]
            full = self.Wb[(nm, l)]
            bf = []
            for K in range((r // 2) // nr2):
                lo, hi = K * nr2, (K + 1) * nr2
                deps = [bh[k] for k in range(lo // (4 * nr1), (hi - 1) // (4 * nr1) + 1)]
                b = P.buf()
                cc(G2, half.ap()[lo:hi, :], full.ap()[K * 2 * nr2:(K + 1) * 2 * nr2, :], deps, [b])
                bf.append(b)
            self.b_W[(nm, l)] = bf
        P.bg = False

    def stage_P(self, l, xsrc, xpred_sb_loader):
        P, c = self.P, self.cfg
        KD, NBo, NBS, NHALO = c.KD, c.NBo, c.NBS, c.NHALO
        Win = self.Wb[("w_in", l)].ap()
        b_Win = self.b_W[("w_in", l)]
        o = c.o
        self.reset_arena()
        hT = self.abf([KD, NBS * T]); b_hT = P.buf("hT")
        WBF_BYTES = 2 * KD * 512 * 2
        wbf_off = self.ab(max(WBF_BYTES, KD * T * 2))
        sq = self.av(wbf_off, [KD, T]); b_sq = P.buf("sq")
        NOST = 8
        ost = [self.abf([T]) for _ in range(NOST)]; b_ost = [P.buf() for _ in range(NOST)]
        hhalo = self.abf([KD, NHALO]); b_hhalo = P.buf("hhalo")
        sqh = self.abf([KD, NHALO]); b_sqh = P.buf()
        xs1 = self.af32([KD, T]); b_xs1 = P.buf()
        xs = [xs1, xs1]; b_xs = [b_xs1, b_xs1]
        xh = self.af32([KD, NHALO]); b_xh = P.buf()
        tmp = self.af32([T]); b_tmp = P.buf()
        NFT = 7
        ft = [self.af32([T + 2]) for _ in range(NFT)]; b_ft = [P.buf() for _ in range(NFT)]
        tab = [self.af32([T]) for _ in range(3)]; b_tab = [P.buf() for _ in range(3)]
        lsb = self.af32([T]); b_lsb = P.buf()
        state = dict(ost=0, ft=0)
        prodh = self.af32([NH, NHALO]); b_prodh = P.buf()

        def next_ost():
            k = state["ost"] % NOST
            state["ost"] += 1
            return ost[k], b_ost[k]

        def next_ft():
            k = state["ft"] % NFT
            state["ft"] += 1
            return ft[k], b_ft[k]

        xpred_sb_loader(xh, b_xh)
        self.norm_tile(xh, b_xh, NHALO, ("g_mix", l), hhalo, b_hhalo, sqh, b_sqh, tmp[:, 0:NHALO], b_tmp)

        kv = self.xch_view("own", 0)
        vv = self.xch_view("own", 1)
        rkv = self.xch_view("own", 2)
        rvv = self.xch_view("own", 3)
        banks = [0, 1, 2, 3, 4, 5]
        scale = HD ** -0.5
        nst = NBo // NBS
        for st in range(nst):
            for bi in range(NBS):
                blk = st * NBS + bi
                s = bi % 2
                P.dma("sp", xs[s], xsrc.ap()[blk].rearrange("p (k t) -> p k t", k=KD), reads=self.b_x[blk],
                      writes=[b_xs[s]])
                self.norm_tile(xs[s], b_xs[s], T, ("g_mix", l), hT[:, :, bi * T:(bi + 1) * T], b_hT, sq, b_sq,
                               tmp, b_tmp)
            P.barrier()

            def blk_of(tt):
                return st * NBS + tt

            def pairs(off):
                return [[(Win, off + (4 * g + j) * 128, 128, hT, b_hT) for j in range(4)] for g in range(NH // 4)]

            def ep_k(g, tt, res):
                for j, (psb, w) in enumerate(res):
                    h = 4 * g + j
                    t_, bt = next_ost()
                    CP(P, "act", t_, self.ps[psb][:, :], [self.b_ps[psb]], [bt])
                    P.dma("act", kv[h, :, blk_of(tt), :], t_, reads=[bt], writes=[self.b_xown])
            self.gemm_fm(KD, NBS, pairs(o["fk"]), ep_k, wbf_off, banks, wreads=[b_Win])

            def mk_ep_v(view):
                def ep_v(g, tc, psb):
                    blk = st * NBS + tc // 4
                    s4 = tc % 4
                    t_, bt = next_ost()
                    CP(P, "act", t_[:, 0:256], self.ps[psb][:, 0:256], [self.b_ps[psb]], [bt])
                    for j in range(2):
                        h = 2 * g + j
                        P.dma("act", view[h, :, blk, s4 * 128:(s4 + 1) * 128], t_[:, j * 128:(j + 1) * 128],
                              reads=[bt], writes=[self.b_xown])
                return ep_v
            self.gemm_tm(Win, KD, hT, b_hT, NBS * 4, o["fv"], HW, mk_ep_v(vv), wbf_off, banks, wreads=[b_Win])
            self.gemm_tm(Win, KD, hT, b_hT, NBS * 4, o["rv"], HW, mk_ep_v(rvv), wbf_off, banks, wreads=[b_Win])

            def ep_fl(g, tt, res):
                psb, w = res[0]
                ACTV(P, lsb[0:NH, :], self.ps[psb][0:NH, :], AF.Exp, [self.b_ps[psb], self.b_vecs], [b_lsb],
                     bias=self.v(("nb_f", l))[0:NH, :], scale=-1.0)
                ACTV(P, lsb[0:NH, :], lsb[0:NH, :], AF.Ln, [b_lsb, self.b_vecs], [b_lsb],
                     bias=self.v("one")[0:NH, :], scale=1.0)
                blk = blk_of(tt)
                P.dma("act", self.ls_own.ap()[:, blk * T:(blk + 1) * T], lsb[0:NH, :], reads=[b_lsb],
                      writes=[self.b_lsown])
            self.gemm_fm(KD, NBS, [[(Win, o["fl"], NH, hT, b_hT)]], ep_fl, wbf_off, banks, wreads=[b_Win])

            def rotary(psb, blk, outs):
                q32, bq = next_ft()
                CP(P, "act", q32[:, 0:T], self.ps[psb][:, :], [self.b_ps[psb]], [bq])
                MM(P, self.ps[6][:, :], self.pm_f[:], q32[:, 0:T], True, True, [bq, self.b_const], [self.b_ps[6]])
                t1, b1 = next_ft()
                TT(P, "pool", t1[:, 0:T], q32[:, 0:T], tab[0], ALU.mult, [bq, b_tab[0]], [b1])
                t2, b2 = next_ft()
                TT(P, "dve", t2[:, 0:T], self.ps[6][:, :], tab[1], ALU.mult, [self.b_ps[6], b_tab[1]], [b2])
                TT(P, "dve", t1[:, 0:T], t1[:, 0:T], t2[:, 0:T], ALU.add, [b1, b2], [b1])
                for dst, kind, bdst in outs:
                    t_, bt = next_ost()
                    if kind == "scale":
                        ACTV(P, t_, t1[:, 0:T], AF.Copy, [b1], [bt], scale=scale)
                    elif kind == "decq":
                        TT(P, "pool", t_, t1[:, 0:T], tab[2], ALU.mult, [b1, b_tab[2]], [bt])
                    else:
                        CP(P, "act", t_, t1[:, 0:T], [b1], [bt])
                    P.dma("sp" if kind == "decq" else "act", dst, t_, reads=[bt], writes=[bdst])

            def load_tabs(blk, h=None):
                P.dma("sp", tab[0], self.rope_d.ap()[0, :, blk * T:(blk + 1) * T], writes=[b_tab[0]])
                P.dma("sp", tab[1], self.rope_d.ap()[1, :, blk * T:(blk + 1) * T], writes=[b_tab[1]])

            def singles(off):
                return [[(Win, off + h * 128, 128, hT, b_hT)] for h in range(NH)]

            def ep_rk(g, tt, res):
                psb, w = res[0]
                blk = blk_of(tt)
                load_tabs(blk)
                rotary(psb, blk, [(rkv[g, :, blk, :], "copy", self.b_xown)])
            self.gemm_fm(KD, NBS, singles(o["rk"]), ep_rk, wbf_off, banks, wreads=[b_Win])

            if st == nst - 1:
                self.exchange(l)

            def ep_q(g, tt, res):
                for j, (psb, w) in enumerate(res):
                    h = 4 * g + j
                    t_, bt = next_ost()
                    ACTV(P, t_, self.ps[psb][:, :], AF.Copy, [self.b_ps[psb]], [bt], scale=scale)
                    P.dma("act", self.qT.ap()[h, blk_of(tt)], t_, reads=[bt], writes=[self.b_q])
            self.gemm_fm(KD, NBS, pairs(o["fq"]), ep_q, wbf_off, banks, wreads=[b_Win])

            def ep_rq(g, tt, res):
                psb, w = res[0]
                blk = blk_of(tt)
                load_tabs(blk)
                P.dma("sp", tab[2], self.decq_d.ap()[g], writes=[b_tab[2]])
                rotary(psb, blk, [(self.rqT.ap()[g, blk], "scale", self.b_rq), (self.rqdT.ap()[g, blk], "decq", self.b_rq)])
            self.gemm_fm(KD, NBS, singles(o["rq"]), ep_rq, wbf_off, banks, wreads=[b_Win])

            def ep_rg(g, tt, res):
                for j, (psb, w) in enumerate(res):
                    h = 4 * g + j
                    t_, bt = next_ost()
                    ACTV(P, t_, self.ps[psb][:, :], AF.Silu, [self.b_ps[psb]], [bt])
                    P.dma("act", self.rgT.ap()[h, blk_of(tt)], t_, reads=[bt], writes=[self.b_rg])
            self.gemm_fm(KD, NBS, pairs(o["rg"]), ep_rg, wbf_off, banks, wreads=[b_Win])

            def ep_gate(g, tt, res):
                for j, (psb, w) in enumerate(res):
                    m = 4 * g + j
                    t_, bt = next_ost()
                    ACTV(P, t_, self.ps[psb][:, :], AF.Sigmoid, [self.b_ps[psb], self.b_vecs], [bt],
                         bias=self.v(("b_gate", l), 1, m), scale=1.0)
                    P.dma("act", self.gates.ap()[m, blk_of(tt)], t_, reads=[bt], writes=[self.b_gates])
            ng = (3 * KD + 3) // 4
            self.gemm_fm(KD, NBS, [[(Win, o["ga"] + (4 * g + j) * 128, 128, hT, b_hT) for j in range(min(4, 3 * KD - 4 * g))]
                                   for g in range(ng)], ep_gate, wbf_off, banks, wreads=[b_Win])


            def ep_conv_halo(g, hb):
                cc_s, bc = next_ft()
                CP(P, "act", cc_s[:, 0:NHALO], hb[1], [self.b_ps[7]], [bc])
                TT(P, "dve", prodh[:, g, :], cc_s[:, 0:NHALO], hb[2], ALU.mult, [bc, self.b_ps[7]], [b_prodh])

            def ep_conv(g, tt, res):
                blk = blk_of(tt)
                (pb, _), (pc, _), (ph, _) = res
                cc_s, bc = next_ft()
                CP(P, "act", cc_s[:, 0:T], self.ps[pc][:, :], [self.b_ps[pc]], [bc])
                prod, bp = next_ft()
                TT(P, "dve", prod[:, 2:T + 2], cc_s[:, 0:T], self.ps[ph][:, :], ALU.mult, [bc, self.b_ps[ph]], [bp])
                CP(P, "pool", prod[:, 0:2], prodh[:, g, 2 * blk:2 * blk + 2], [b_prodh], [bp])
                y, by = next_ft()
                cw = lambda k: self.v(("conv_w", l), 1, k * NH + g)
                ACTV(P, y[:, 0:T], prod[:, 2:T + 2], AF.Copy, [bp, self.b_vecs], [by], scale=cw(2))
                STT(P, "dve", y[:, 0:T], prod[:, 1:T + 1], cw(1), y[:, 0:T], ALU.mult, ALU.add, [bp, by, self.b_vecs], [by])
                STT(P, "dve", y[:, 0:T], prod[:, 0:T], cw(0), y[:, 0:T], ALU.mult, ALU.add, [bp, by, self.b_vecs], [by])
                t_, bt = next_ost()
                TT(P, "dve", t_, y[:, 0:T], self.ps[pb][:, :], ALU.mult, [by, self.b_ps[pb]], [bt])
                P.dma("sp", self.convT.ap()[g, blk], t_, reads=[bt], writes=[self.b_conv])
            grp = [[(Win, o["cb"] + g * 128, 128, hT, b_hT), (Win, o["cc"] + g * 128, 128, hT, b_hT),
                    (Win, o["ch"] + g * 128, 128, hT, b_hT)] for g in range(NH)]
            self.gemm_fm(KD, NBS, grp, ep_conv, wbf_off, banks, halo=hhalo, b_halo=b_hhalo, nh=NHALO,
                         ep_halo=ep_conv_halo, wreads=[b_Win])
            P.barrier()

    def exchange(self, l):
        self.pair_gather(self.ls_own.ap(), self.ls_all.ap(), [self.b_lsown], [self.b_lsall])
        for k in range(self.NCHK):
            self.pair_gather(self.xch_own.ap()[k * self.RC:(k + 1) * self.RC, :], self.xch_all.ap()[k],
                             [self.b_xown], [self.b_xallk[k]])

    def compute_c(self, l):
        P, c = self.P, self.cfg
        S, NBo, NCH = c.S, c.NBo, c.NCH
        self.reset_arena()
        spg = self.af32([S]); b_spg = P.buf()
        csum = self.af32([S]); b_csum = P.buf()
        ones8 = self.af32([T]); b_o8 = P.buf()
        tmpc = self.af32([NBo * T]); b_tmpc = P.buf()
        if not hasattr(self, "negc"):
            self.negc = P.sbuf("negc", [128, NCH * NH], F32)
            self.b_negc = P.buf()
        MEMSET(P, "dve", ones8[0:NH, :], 1.0, [b_o8])
        spv = spg[0:NH, :].rearrange("p (i r t) -> p i r t", r=2, t=T)
        for r in range(2):
            P.dma("sp", spv[:, :, r, :], self.ls_all.ap()[r * NH:(r + 1) * NH, :].rearrange("p (i t) -> p i t", t=T),
                  reads=[self.b_lsall], writes=[b_spg])
        for n in range(S // T):
            init = 0.0 if n == 0 else csum[0:NH, n * T - 1:n * T]
            sl = slice(n * T, (n + 1) * T)
            P.op("dve", (lambda o_, d1, ini: (lambda e: e.tensor_tensor_scan(o_, ones8[0:NH, :], d1, ini, ALU.mult, ALU.add)))(
                csum[0:NH, sl], spg[0:NH, sl], init), [b_spg, b_o8, b_csum], [b_csum])
        assert NCH * NH <= T
        for n in range(NCH):
            MM(P, self.ps[7][:, n * NH:(n + 1) * NH], csum[0:NH, n * 128:(n + 1) * 128], self.ident_f[0:NH, 0:NH],
               True, True, [b_csum, self.b_const], [self.b_ps[7]])
        CP(P, "act", self.negc[:, :], self.ps[7][:, 0:NCH * NH], [self.b_ps[7]], [self.b_negc])
        cv = csum[0:NH, :].rearrange("p (i r t) -> p i r t", r=2, t=T)
        tv = tmpc[0:NH, :].rearrange("p (i t) -> p i t", t=T)
        TS(P, "dve", tv, cv[:, :, 0, :], self.v("omp")[0:NH, :], None, ALU.mult, None, [b_csum, self.b_vecs], [b_tmpc])
        STT(P, "dve", tv, cv[:, :, 1, :], self.v("pf")[0:NH, :], tv, ALU.mult, ALU.add, [b_csum, b_tmpc, self.b_vecs], [b_tmpc])
        TS(P, "dve", tmpc[0:NH, :], tmpc[0:NH, :], -1.0, None, ALU.mult, None, [b_tmpc], [b_tmpc])
        P.dma("sp", self.cq_d.ap()[:, :], tmpc[0:NH, :], reads=[b_tmpc], writes=[self.b_cq])
        P.barrier()

    def stage_A(self, l):
        P, c = self.P, self.cfg
        NBo, NCH = c.NBo, c.NCH
        self.reset_arena()
        kres = [self.abf([2 * NBo * T]) for _ in range(2)]; b_k = [P.buf(), P.buf()]
        vres = [self.abf([NCH, 128]) for _ in range(2)]; b_v = [P.buf(), P.buf()]
        qt = [self.abf([T]) for _ in range(2)]; b_qt = [P.buf(), P.buf()]
        cqb = [self.af32([T]) for _ in range(2)]; b_cqb = [P.buf(), P.buf()]
        NL = 4
        LA = 2
        lg = [self.af32([T]) for _ in range(NL)]; b_lg = [P.buf() for _ in range(NL)]
        pt = [self.abf([T]) for _ in range(NL)]; b_pt = [P.buf() for _ in range(NL)]
        rec = self.af32([T]); b_rec = P.buf()
        ot = [self.abf([T]) for _ in range(2)]; b_ot = [P.buf(), P.buf()]
        its = []
        qi = 0
        for h in range(NH):
            for i in range(NBo):
                nn = 4 * (2 * i + 2)
                for n in range(nn):
                    its.append((h, i, n, nn, qi % 2))
                qi += 1

        def s_phase(k):
            h, i, n, nn, qs = its[k]
            s = h % 2
            if n == 0 and i == 0:
                kview = kres[s].rearrange("p (i r t) -> p i r t", r=2, t=T)
                vview = vres[s].rearrange("p (i r s) e -> p i r (s e)", r=2, s=4)
                for r in range(2):
                    ka, kb_ = self.xall(0, r, h)
                    va, vb_ = self.xall(1, r, h)
                    P.dma("sp", kview[:, :, r, :], ka, reads=[kb_], writes=[b_k[s]])
                    P.dma("sp", vview[:, :, r, :], va, reads=[vb_], writes=[b_v[s]])
            if n == 0:
                P.dma("sp", qt[qs], self.qT.ap()[h, i], reads=[self.b_q], writes=[b_qt[qs]])
                P.dma("sp", cqb[qs], self.cq_d.ap()[h:h + 1, i * T:(i + 1) * T].broadcast_to([128, T]),
                      reads=[self.b_cq], writes=[b_cqb[qs]])
            sb = k % 3
            ls_ = k % NL
            masked = n >= 8 * i
            MM(P, self.ps[sb][:, :], kres[s][:, n * 128:(n + 1) * 128], qt[qs], True, not masked,
               [b_k[s], b_qt[qs]], [self.b_ps[sb]])
            if masked:
                MM(P, self.ps[sb][:, :], self.ident_b[:], self.masks[:, n - 8 * i, :], False, True,
                   [self.b_const], [self.b_ps[sb]])
            STT(P, "dve", lg[ls_], self.ps[sb][:, :], self.negc[:, n * NH + h:n * NH + h + 1], cqb[qs],
                ALU.add, ALU.add, [self.b_ps[sb], self.b_negc, b_cqb[qs]], [b_lg[ls_]])
            ACTV(P, pt[ls_], lg[ls_], AF.Exp, [b_lg[ls_]], [b_pt[ls_]])

        def pv_phase(k):
            h, i, n, nn, qs = its[k]
            s = h % 2
            ls_ = k % NL
            ob, lb = 3 + qs, 5 + qs
            MM(P, self.ps[ob][:, :], vres[s][:, n, :], pt[ls_], n == 0, n == nn - 1, [b_v[s], b_pt[ls_]],
               [self.b_ps[ob]])
            MM(P, self.ps[lb][:, :], self.ones[:], pt[ls_], n == 0, n == nn - 1, [b_pt[ls_], self.b_const],
               [self.b_ps[lb]])
            if n == nn - 1:
                P.op("dve", (lambda o_, i_: (lambda e: e.reciprocal(o_, i_)))(rec, self.ps[lb][:, :]),
                     [self.b_ps[lb]], [b_rec])
                TT(P, "dve", ot[qs], self.ps[ob][:, :], rec, ALU.mult, [self.b_ps[ob], b_rec], [b_ot[qs]])
                P.dma("sp", self.attnT.ap()[h, i], ot[qs], reads=[b_ot[qs]], writes=[self.b_attn])

        for k in range(len(its) + LA):
            if k < len(its):
                s_phase(k)
            if k - LA >= 0:
                pv_phase(k - LA)
        P.barrier()

    def stage_R(self, l):
        P, c = self.P, self.cfg
        NBo, NCH = c.NBo, c.NCH
        self.reset_arena()
        krT = [self.abf([2 * NBo * T]) for _ in range(2)]; b_kr = [P.buf(), P.buf()]
        vres = [self.abf([NCH, 128]) for _ in range(2)]; b_v = [P.buf(), P.buf()]
        kown = [self.abf([NBo, T]) for _ in range(2)]; b_ko = [P.buf(), P.buf()]
        vown = [self.abf([NBo * 4, 128]) for _ in range(2)]; b_vo = [P.buf(), P.buf()]
        dm4 = [self.af32([4, 128]) for _ in range(2)]; b_dm = [P.buf(), P.buf()]
        dk = self.af32([NH]); b_dk = P.buf()
        P.dma("sp", dk, self.dkT_d.ap(), writes=[b_dk])
        Sst = self.af32([128]); b_S = P.buf()
        Sown = self.abf([NBo * 4, 128]); b_So = P.buf()
        NK = 3
        kdec = [self.abf([128]) for _ in range(NK)]; b_kd = [P.buf() for _ in range(NK)]
        qt = [self.abf([T]) for _ in range(2)]; b_qt = [P.buf(), P.buf()]
        qd = [self.abf([T]) for _ in range(2)]; b_qd = [P.buf(), P.buf()]
        rgt = [self.abf([T]) for _ in range(2)]; b_rg = [P.buf(), P.buf()]
        attm = [self.abf([4, 128]) for _ in range(2)]; b_am = [P.buf(), P.buf()]
        rb = self.abf([T]); b_rb = P.buf()
        rsq = self.abf([T]); b_rsq = P.buf()
        mean = self.af32([T]); b_mean = P.buf()
        msq = self.af32([T]); b_msq = P.buf()
        var = self.af32([T]); b_var = P.buf()
        cen = self.af32([T]); b_cen = P.buf()
        yo = [self.abf([T]) for _ in range(2)]; b_yo = [P.buf(), P.buf()]
        gam = 1.0 - 2.0 ** (-5.0 - np.arange(NH, dtype=np.float32))
        cdec = np.exp(np.log(gam.astype(np.float32)).astype(np.float32) * 128.0).astype(np.float32)
        it = 0
        qi = 0
        for h in range(NH):
            s = h % 2
            kview = krT[s].rearrange("p (i r t) -> p i r t", r=2, t=T)
            vview = vres[s].rearrange("p (i r s) e -> p i r (s e)", r=2, s=4)
            for r in range(2):
                ka, kb_ = self.xall(2, r, h)
                va, vb_ = self.xall(3, r, h)
                P.dma("sp", kview[:, :, r, :], ka, reads=[kb_], writes=[b_kr[s]])
                P.dma("sp", vview[:, :, r, :], va, reads=[vb_], writes=[b_v[s]])
            P.dma("sp", kown[s], self.xch_view("own", 2)[h], reads=[self.b_xown], writes=[b_ko[s]])
            P.dma("sp", vown[s].rearrange("p (i s) e -> p i (s e)", s=4), self.xch_view("own", 3)[h],
                  reads=[self.b_xown], writes=[b_vo[s]])
            for q4 in range(4):
                P.dma("sp", dm4[s][:, q4, :], self.dmaskT_d.ap()[h], writes=[b_dm[s]])
            MEMSET(P, "dve", Sst, 0.0, [b_S])

            def tpose(n, itn):
                kk, pb = itn % NK, itn % 3
                MM(P, self.ps[pb][:, 0:128], krT[s][:, n * 128:(n + 1) * 128], self.ident_b[:], True, True,
                   [b_kr[s], self.b_const], [self.b_ps[pb]])
                ACTV(P, kdec[kk], self.ps[pb][:, 0:128], AF.Copy, [self.b_ps[pb], b_dk], [b_kd[kk]], scale=dk[:, h:h + 1])

            def contrib(n, itn):
                kk, cb = itn % NK, 3
                MM(P, self.ps[cb][:, (itn % 4) * 128:(itn % 4 + 1) * 128], kdec[kk], vres[s][:, n, :], True, True,
                   [b_kd[kk], b_v[s]], [self.b_ps[cb]])
                return self.ps[cb][:, (itn % 4) * 128:(itn % 4 + 1) * 128]

            tpose(0, it)
            for n in range(NCH):
                i, r, s4 = n // 8, (n // 4) % 2, n % 4
                slot = i * 4 + s4
                if r == 0:
                    TS(P, "dve", Sown[:, slot, :], Sst, self.v("omp"), None, ALU.mult, None, [b_S, self.b_vecs], [b_So])
                else:
                    STT(P, "dve", Sown[:, slot, :], Sst, self.v("pf"), Sown[:, slot, :], ALU.mult, ALU.add,
                        [b_S, b_So, self.b_vecs], [b_So])
                if n == NCH - 1:
                    break
                if n + 1 < NCH - 1:
                    tpose(n + 1, it + 1)
                cps = contrib(n, it)
                it += 1
                STT(P, "dve", Sst, Sst, float(cdec[h]), cps, ALU.mult, ALU.add, [b_S, self.b_ps[3]], [b_S])
            for i in range(NBo):
                qs = qi % 2
                qi += 1
                P.dma("sp", qt[qs], self.rqT.ap()[h, i], reads=[self.b_rq], writes=[b_qt[qs]])
                P.dma("sp", qd[qs], self.rqdT.ap()[h, i], reads=[self.b_rq], writes=[b_qd[qs]])
                P.dma("sp", rgt[qs], self.rgT.ap()[h, i], reads=[self.b_rg], writes=[b_rg[qs]])
                ab = 4
                for s4 in range(4):
                    MM(P, self.ps[ab][:, s4 * 128:(s4 + 1) * 128], kown[s][:, i, s4 * 128:(s4 + 1) * 128],
                       qt[qs][:, s4 * 128:(s4 + 1) * 128], True, True, [b_ko[s], b_qt[qs]], [self.b_ps[ab]])
                TT(P, "dve", attm[qs].rearrange("p a b -> p (a b)"), self.ps[ab][:, :], dm4[s].rearrange("p a b -> p (a b)"),
                   ALU.mult, [self.b_ps[ab], b_dm[s]], [b_am[qs]])
                rbk = 5
                for s4 in range(4):
                    MM(P, self.ps[rbk][:, s4 * 128:(s4 + 1) * 128], vown[s][:, i * 4 + s4, :], attm[qs][:, s4, :],
                       True, False, [b_vo[s], b_am[qs]], [self.b_ps[rbk]])
                    MM(P, self.ps[rbk][:, s4 * 128:(s4 + 1) * 128], Sown[:, i * 4 + s4, :],
                       qd[qs][:, s4 * 128:(s4 + 1) * 128], False, True, [b_So, b_qd[qs]], [self.b_ps[rbk]])
                rps = self.ps[rbk][:, :]
                CP(P, "act", rb, rps, [self.b_ps[rbk]], [b_rb])
                ACTV(P, rsq, rps, AF.Square, [self.b_ps[rbk]], [b_rsq])
                MM(P, self.ps[6][:, :], self.o128[:], rb, True, True, [b_rb, self.b_const], [self.b_ps[6]])
                MM(P, self.ps[7][:, :], self.o128[:], rsq, True, True, [b_rsq, self.b_const], [self.b_ps[7]])
                CP(P, "act", mean, self.ps[6][:, :], [self.b_ps[6]], [b_mean])
                TT(P, "pool", msq, mean, mean, ALU.mult, [b_mean], [b_msq])
                TT(P, "dve", var, self.ps[7][:, :], msq, ALU.subtract, [self.b_ps[7], b_msq], [b_var])
                ACTV(P, var, var, AF.Ln, [b_var, self.b_vecs], [b_var], bias=self.v("eps"), scale=1.0)
                ACTV(P, var, var, AF.Exp, [b_var], [b_var], scale=-0.5)
                TT(P, "dve", cen, rps, mean, ALU.subtract, [self.b_ps[rbk], b_mean], [b_cen])
                TT(P, "dve", cen, cen, var, ALU.mult, [b_cen, b_var], [b_cen])
                ACTV(P, cen, cen, AF.Identity, [b_cen, self.b_vecs], [b_cen], bias=self.v(("gn_b", l), 1, h),
                     scale=self.v(("gn_g", l), 1, h))
                TT(P, "pool", yo[qs], cen, rgt[qs], ALU.mult, [b_cen, b_rg[qs]], [b_yo[qs]])
                P.dma("sp", self.retoT.ap()[h, i], yo[qs], reads=[b_yo[qs]], writes=[self.b_reto])
        P.barrier()

    def make_resid_ep(self, xsrc, xdst, blk_of, chunk_of, temps, save_tail):
        P, c = self.P, self.cfg
        xo, b_xo = temps

        def ep(g, tt, res):
            blk = blk_of(tt)
            for j, (psb, w) in enumerate(res):
                m = chunk_of(g, j)
                k = self._xo_rr % len(xo)
                self._xo_rr += 1
                P.dma("sp", xo[k], xsrc.ap()[blk][:, m * T:(m + 1) * T], reads=[self.b_x[blk][m]], writes=[b_xo[k]])
                TT(P, "dve", xo[k], xo[k], self.ps[psb][:, :], ALU.add, [b_xo[k], self.b_ps[psb]], [b_xo[k]])
                if save_tail:
                    CP(P, "pool", self.xtail[:, m, blk, :], xo[k][:, T - 2:T], [b_xo[k]], [self.b_xtail])
                P.dma(self.store_q, xdst.ap()[blk][:, m * T:(m + 1) * T], xo[k], reads=[b_xo[k]], writes=[self.b_x[blk][m]])
        return ep

    def exchange_tails(self, dst, b_dst):
        P, c = self.P, self.cfg
        KD, NBo = c.KD, c.NBo
        n = KD * NBo * 2
        P.dma("sp", self.tail_own.ap()[:, :], self.xtail[:, :, :, :].rearrange("p k i t -> p (k i t)"),
              reads=[self.b_xtail], writes=[self.b_tailown])
        self.pair_gather(self.tail_own.ap(), self.tail_all.ap(), [self.b_tailown], [self.b_tailall])
        c0, c1 = self.cand
        b_c = self.b_cand
        P.dma("sp", c0[:, :, :, :], self.tail_all.ap()[0:128, :].rearrange("p (k i t) -> p k i t", k=KD, t=2),
              reads=[self.b_tailall], writes=[b_c])
        MEMSET(P, "dve", c1[:, :, 0:1, :], 0.0, [b_c])
        if NBo > 1:
            g1 = self.tail_all.ap()[128:256, :].rearrange("p (k i t) -> p k i t", k=KD, t=2)
            P.dma("sp", c1[:, :, 1:NBo, :], g1[:, :, 0:NBo - 1, :], reads=[self.b_tailall], writes=[b_c])
        dv = dst.rearrange("p k (i t) -> p k i t", t=2)
        TS(P, "dve", dv, c0[:, :, :, :], self.v("pf"), None, ALU.mult, None, [b_c, self.b_vecs], [b_dst])
        STT(P, "dve", dv, c1[:, :, :, :], self.v("omp"), dv, ALU.mult, ALU.add, [b_c, b_dst, self.b_vecs], [b_dst])

    def alloc_persist(self):
        P, c = self.P, self.cfg
        self.xtail = P.sbuf("xtail", [128, c.KD, c.NBo, 2], F32); self.b_xtail = P.buf()
        self.cand = (P.sbuf("cand0", [128, c.KD, c.NBo, 2], F32), P.sbuf("cand1", [128, c.KD, c.NBo, 2], F32))
        self.b_cand = P.buf()
        self.xpredF = P.sbuf("xpredF", [128, c.KD, c.NHALO], F32); self.b_xpredF = P.buf()
        self.kmT = P.sbuf("kmT", [128, 4, c.MEM], BF16); self.vm = P.sbuf("vm", [128, c.MEM // 128, 512], BF16)
        self.b_km = P.buf(); self.b_vm = P.buf()
        self._xo_rr = 0

    def prep_mem(self, l):
        P, c = self.P, self.cfg
        KD, MEM = c.KD, c.MEM
        self.reset_arena()
        xs = self.af32([KD, MEM]); b_xs = P.buf()
        sq = self.abf([KD, MEM]); b_sq = P.buf()
        tmp = self.af32([MEM]); b_tmp = P.buf()
        mn = self.abf([KD, MEM]); b_mn = P.buf()
        wk = self.abf([KD, 1024]); b_wk = P.buf()
        P.dma("sp", xs, self.memT.ap().rearrange("p (k m) -> p k m", k=KD), writes=[b_xs])
        self.norm_tile(xs, b_xs, MEM, ("g_mem", l), mn, b_mn, sq, b_sq, tmp, b_tmp)
        Wv = self.Wb[("w_ckv", l)].ap().rearrange("(c p) n -> p c n", p=128)
        P.dma("sp", wk, Wv, reads=self._flat([self.b_W[("w_ckv", l)]]), writes=[b_wk])
        for hh in range(4):
            pb = hh % 2
            for k in range(KD):
                MM(P, self.ps[pb][:, 0:MEM], wk[:, k, hh * 128:(hh + 1) * 128], mn[:, k, :], k == 0, k == KD - 1,
                   [b_wk, b_mn], [self.b_ps[pb]])
            CP(P, "act", self.kmT[:, hh, :], self.ps[pb][:, 0:MEM], [self.b_ps[pb]], [self.b_km])
        for mc in range(MEM // 128):
            pb = 2 + mc % 2
            for k in range(KD):
                MM(P, self.ps[pb][:, :], mn[:, k, mc * 128:(mc + 1) * 128], wk[:, k, 512:1024], k == 0, k == KD - 1,
                   [b_wk, b_mn], [self.b_ps[pb]])
            CP(P, "act", self.vm[:, mc, :], self.ps[pb][:, :], [self.b_ps[pb]], [self.b_vm])
        P.barrier()

    def stage_M(self, l, st, xsrc, xdst):
        P, c = self.P, self.cfg
        KD, NBM = c.KD, self.NBM
        TSM = NBM * T
        self.reset_arena()
        X = [self.abf([NH, TSM]) for _ in range(3)]; b_X = [P.buf() for _ in range(3)]
        merged = self.abf([KD, TSM]); b_mg = P.buf()
        wbf_off = self.ab(max(2 * NH * 384 * 2, 2 * KD * 256 * 2))
        NG_ = 6
        gt = [self.abf([T]) for _ in range(NG_)]; b_gt = [P.buf() for _ in range(NG_)]
        t1 = [self.af32([T]) for _ in range(2)]; b_t1 = [P.buf(), P.buf()]
        t2 = [self.af32([T]) for _ in range(2)]; b_t2 = [P.buf(), P.buf()]
        xo = [self.af32([T]) for _ in range(3)]; b_xo = [P.buf() for _ in range(3)]
        srcs = [(self.attnT, self.b_attn), (self.convT, self.b_conv), (self.retoT, self.b_reto)]
        for br, (src, bs) in enumerate(srcs):
            for bi in range(NBM):
                blk = st * NBM + bi
                P.dma("sp", X[br][:, :, bi * T:(bi + 1) * T], src.ap()[:, blk].rearrange("h p t -> p h t"),
                      reads=[bs], writes=[b_X[br]])
        Wn = ["w_fox_o", "w_conv_o", "w_ret_o"]
        groups = [[(self.Wb[(Wn[br], l)].ap(), m * 128, 128, X[br], b_X[br]) for br in range(3)] for m in range(KD)]
        cnt = [0]

        def ep(g, tt, res):
            blk = st * NBM + tt
            k = cnt[0] % 2
            cnt[0] += 1
            gts = []
            for br in range(3):
                q = (cnt[0] * 3 + br) % NG_
                P.dma("sp", gt[q], self.gates.ap()[br * KD + g, blk], reads=[self.b_gates], writes=[b_gt[q]])
                gts.append(q)
            TT(P, "dve", t1[k], self.ps[res[0][0]][:, :], gt[gts[0]], ALU.mult, [self.b_ps[res[0][0]], b_gt[gts[0]]], [b_t1[k]])
            TT(P, "dve", t2[k], self.ps[res[1][0]][:, :], gt[gts[1]], ALU.mult, [self.b_ps[res[1][0]], b_gt[gts[1]]], [b_t2[k]])
            TT(P, "pool", t1[k], t1[k], t2[k], ALU.add, [b_t1[k], b_t2[k]], [b_t1[k]])
            TT(P, "dve", t2[k], self.ps[res[2][0]][:, :], gt[gts[2]], ALU.mult, [self.b_ps[res[2][0]], b_gt[gts[2]]], [b_t2[k]])
            TT(P, "pool", merged[:, g, tt * T:(tt + 1) * T], t1[k], t2[k], ALU.add, [b_t1[k], b_t2[k]], [b_mg])
        self.gemm_fm(NH, NBM, groups, ep, wbf_off, [0, 1, 2, 3, 4, 5], wreads=[self.b_W[(w_, l)] for w_ in Wn])
        Wo = self.Wb[("w_out", l)].ap()
        groups = [[(Wo, (2 * g + j) * 128, 128, merged, b_mg) for j in range(min(2, KD - 2 * g))] for g in range((KD + 1) // 2)]
        ep2 = self.make_resid_ep(xsrc, xdst, lambda tt: st * NBM + tt, lambda g, j: 2 * g + j, (xo, b_xo), False)
        self.gemm_fm(KD, NBM, groups, ep2, wbf_off, [0, 1, 2, 3, 4, 5], wreads=[self.b_W[("w_out", l)]])
        P.barrier()

    def stage_C(self, l, st, xr):
        P, c = self.P, self.cfg
        KD, NBM, MEM = c.KD, self.NBM, c.MEM
        TSM = NBM * T
        self.reset_arena()
        xs = self.af32([KD, T]); b_xs = P.buf()
        sq = self.abf([KD, T]); b_sq = P.buf()
        tmp = self.af32([T]); b_tmp = P.buf()
        hc = self.abf([KD, TSM]); b_hc = P.buf()
        qc = self.abf([4, TSM]); b_qc = P.buf()
        co = self.abf([4, TSM]); b_co = P.buf()
        wbf_off = self.ab(2 * KD * 256 * 2)
        pt = [self.abf([T]) for _ in range(3)]; b_pt = [P.buf() for _ in range(3)]
        rec = self.af32([T]); b_rec = P.buf()
        xo = [self.af32([T]) for _ in range(3)]; b_xo = [P.buf() for _ in range(3)]
        for bi in range(NBM):
            blk = st * NBM + bi
            P.dma("sp", xs, xr.ap()[blk].rearrange("p (k t) -> p k t", k=KD), reads=self.b_x[blk], writes=[b_xs])
            self.norm_tile(xs, b_xs, T, ("g_cross", l), hc[:, :, bi * T:(bi + 1) * T], b_hc, sq, b_sq, tmp, b_tmp)
        Wq = self.Wb[("w_cq", l)].ap()
        scale = HD ** -0.5

        def ep_q(g, tt, res):
            for j, (psb, w) in enumerate(res):
                ACTV(P, qc[:, 2 * g + j, tt * T:(tt + 1) * T], self.ps[psb][:, :], AF.Copy, [self.b_ps[psb]], [b_qc], scale=scale)
        self.gemm_fm(KD, NBM, [[(Wq, (2 * g + j) * 128, 128, hc, b_hc) for j in range(2)] for g in range(2)], ep_q,
                     wbf_off, [0, 1, 2], wreads=[self.b_W[("w_cq", l)]])
        nmc = MEM // 128
        NPT = 4
        ptc = [self.abf([T]) for _ in range(NPT)]; b_ptc = [P.buf() for _ in range(NPT)]
        itc = 0
        jj = 0
        for hh in range(4):
            for tt in range(NBM):
                ob, lb = 3 + jj % 2, 5 + jj % 2
                jj += 1
                pqs = []
                for mc in range(nmc):
                    sb = itc % 3
                    pq = itc % NPT
                    itc += 1
                    MM(P, self.ps[sb][:, :], self.kmT[:, hh, mc * 128:(mc + 1) * 128], qc[:, hh, tt * T:(tt + 1) * T],
                       True, True, [self.b_km, b_qc], [self.b_ps[sb]])
                    ACTV(P, ptc[pq], self.ps[sb][:, :], AF.Exp, [self.b_ps[sb]], [b_ptc[pq]])
                    pqs.append(pq)
                for mc, pq in enumerate(pqs):
                    MM(P, self.ps[ob][:, :], self.vm[:, mc, hh * 128:(hh + 1) * 128], ptc[pq], mc == 0, mc == nmc - 1,
                       [self.b_vm, b_ptc[pq]], [self.b_ps[ob]])
                    MM(P, self.ps[lb][:, :], self.ones[:], ptc[pq], mc == 0, mc == nmc - 1, [b_ptc[pq], self.b_const],
                       [self.b_ps[lb]])
                P.op("dve", (lambda o_, i_: (lambda e: e.reciprocal(o_, i_)))(rec, self.ps[lb][:, :]), [self.b_ps[lb]], [b_rec])
                TT(P, "dve", co[:, hh, tt * T:(tt + 1) * T], self.ps[ob][:, :], rec, ALU.mult, [self.b_ps[ob], b_rec], [b_co])
        Wco = self.Wb[("w_co", l)].ap()
        groups = [[(Wco, (2 * g + j) * 128, 128, co, b_co) for j in range(min(2, KD - 2 * g))] for g in range((KD + 1) // 2)]
        ep2 = self.make_resid_ep(xr, xr, lambda tt: st * NBM + tt, lambda g, j: 2 * g + j, (xo, b_xo), True)
        self.gemm_fm(4, NBM, groups, ep2, wbf_off, [0, 1, 2], wreads=[self.b_W[("w_co", l)]])
        P.barrier()

    def stage_F(self, l, st, xr):
        P, c = self.P, self.cfg
        KD, NBM, FC, NHALO = c.KD, self.NBM, c.FC, c.NHALO
        TSM = NBM * T
        self.reset_arena()
        hf = self.abf([KD, TSM]); b_hf = P.buf()
        hh_ = self.abf([KD, NHALO]); b_hh = P.buf()
        sqh = self.abf([KD, NHALO]); b_sqh = P.buf()
        tmp = self.af32([T]); b_tmp = P.buf()
        wbf_off = self.ab(max(2 * KD * 256 * 2, 2 * FC * 128 * 2))
        NT_ = 4
        tu = [self.af32([T + 2]) for _ in range(NT_)]; b_tu = [P.buf() for _ in range(NT_)]
        ty = [self.af32([T]) for _ in range(NT_)]; b_ty = [P.buf() for _ in range(NT_)]
        uh = [self.af32([2, NHALO]) for _ in range(2)]; b_uh = [P.buf(), P.buf()]
        xo = [self.af32([T]) for _ in range(3)]; b_xo = [P.buf() for _ in range(3)]
        act_off = self.ab(max(FC * TSM * 2, KD * T * 4 + KD * T * 2))
        act = self.av(act_off, [FC, TSM]); b_act = P.buf()
        xs = self.fv(act_off, [KD, T]); b_xs = P.buf()
        sq = self.av(act_off + KD * T * 4, [KD, T]); b_sq = P.buf()
        self.norm_tile(self.xpredF[:, :, :], self.b_xpredF, NHALO, ("g_ffn", l), hh_, b_hh, sqh, b_sqh, tmp[:, 0:NHALO], b_tmp)
        for bi in range(NBM):
            blk = st * NBM + bi
            P.dma("sp", xs, xr.ap()[blk].rearrange("p (k t) -> p k t", k=KD), reads=self.b_x[blk], writes=[b_xs])
            self.norm_tile(xs, b_xs, T, ("g_ffn", l), hf[:, :, bi * T:(bi + 1) * T], b_hf, sq, b_sq, tmp, b_tmp)
        P.barrier()
        Wu = self.Wb[("w_up", l)].ap()
        groups = [[(Wu, fa * 128, 128, hf, b_hf), (Wu, c.DFF + fa * 128, 128, hf, b_hf)] for fa in range(FC)]
        rr = [0, 0]
        cur_uh = [None]

        def ep_halo(g, hb):
            k = rr[1] % 2
            rr[1] += 1
            CP(P, "act", uh[k][:, 0, :], hb[0], [self.b_ps[7]], [b_uh[k]])
            CP(P, "act", uh[k][:, 1, :], hb[1], [self.b_ps[7]], [b_uh[k]])
            cur_uh[0] = k

        def fcw(k3, ch):
            return self.v(("fcw", l), 1, k3 * 2 * FC + ch)

        def ep(g, tt, res):
            blk = st * NBM + tt
            ku = cur_uh[0]
            ys = []
            for half, (psb, w) in enumerate(res):
                ch = g if half == 0 else FC + g
                k = rr[0] % NT_
                rr[0] += 1
                CP(P, "act", tu[k][:, 2:T + 2], self.ps[psb][:, :], [self.b_ps[psb]], [b_tu[k]])
                CP(P, "pool", tu[k][:, 0:2], uh[ku][:, half, 2 * blk:2 * blk + 2], [b_uh[ku]], [b_tu[k]])
                ACTV(P, ty[k], tu[k][:, 2:T + 2], AF.Identity, [b_tu[k], self.b_vecs], [b_ty[k]],
                     bias=self.v(("fcb", l), 1, ch), scale=fcw(2, ch))
                STT(P, "dve", ty[k], tu[k][:, 1:T + 1], fcw(1, ch), ty[k], ALU.mult, ALU.add, [b_tu[k], b_ty[k], self.b_vecs], [b_ty[k]])
                STT(P, "dve", ty[k], tu[k][:, 0:T], fcw(0, ch), ty[k], ALU.mult, ALU.add, [b_tu[k], b_ty[k], self.b_vecs], [b_ty[k]])
                ys.append(k)
            ka, kg = ys
            ACTV(P, ty[kg], ty[kg], AF.Silu, [b_ty[kg]], [b_ty[kg]])
            TT(P, "pool", act[:, g, tt * T:(tt + 1) * T], ty[ka], ty[kg], ALU.mult, [b_ty[ka], b_ty[kg]], [b_act])
        self.gemm_fm(KD, NBM, groups, ep, wbf_off, [0, 1, 2, 3, 4, 5], halo=hh_, b_halo=b_hh, nh=NHALO, ep_halo=ep_halo, wreads=[self.b_W[("w_up", l)]])
        Wd = self.Wb[("w_down", l)].ap()
        groups = [[(Wd, m * 128, 128, act, b_act)] for m in range(KD)]
        ep2 = self.make_resid_ep(xr, xr, lambda tt: st * NBM + tt, lambda g, j: g, (xo, b_xo), True)
        self.gemm_fm(FC, NBM, groups, ep2, wbf_off, [0, 1, 2, 3, 4, 5], wreads=[self.b_W[("w_down", l)]])
        P.barrier()

    def final_norm(self, xr):
        P, c = self.P, self.cfg
        KD, NBo = c.KD, c.NBo
        self.reset_arena()
        xs = [self.af32([KD, T]) for _ in range(2)]; b_xs = [P.buf(), P.buf()]
        yo = [self.af32([KD, T]) for _ in range(2)]; b_yo = [P.buf(), P.buf()]
        sq = self.abf([KD, T]); b_sq = P.buf()
        tmp = self.af32([T]); b_tmp = P.buf()
        for blk in range(NBo):
            s = blk % 2
            P.dma("sp", xs[s], xr.ap()[blk].rearrange("p (k t) -> p k t", k=KD), reads=self.b_x[blk], writes=[b_xs[s]])
            self.norm_tile(xs[s], b_xs[s], T, "g_final", yo[s], b_yo[s], sq, b_sq, tmp, b_tmp)
            P.dma("pool", xr.ap()[blk].rearrange("p (k t) -> p k t", k=KD), yo[s], reads=[b_yo[s]], writes=self.b_x[blk])

    def build(self, sharded, upto="Z"):
        P, c = self.P, self.cfg
        self.NBM = min(2, c.NBo)
        self.alloc_persist()
        self.prep_weights(sharded)
        SPLIT = False
        self.gather_weights(0, ["w_in"] if SPLIT else None)
        P.pool_alt = "dve"
        self.store_q = "sp"
        xsrc = self.xT
        xr = self.out
        xh_holder = {}
        for l in range(c.L):
            if l == 0:
                def loader(dst, buf):
                    P.dma("sp", dst, self.xpred0.ap().rearrange("p (k n) -> p k n", k=c.KD), writes=[buf])
            else:
                def loader(dst, buf):
                    self.exchange_tails(dst, buf)
            self.stage_P(l, xsrc, loader)
            if l == 0 and SPLIT:
                self.gather_weights(0, [n_ for n_ in self.weight_shapes() if n_ != "w_in"])
            if l + 1 < c.L:
                self.gather_weights(l + 1)
            if upto == "P": return
            self.compute_c(l)
            self.stage_A(l)
            if upto == "A": return
            self.stage_R(l)
            if upto == "R": return
            self.prep_mem(l)
            nst = c.NBo // self.NBM
            for st in range(nst):
                self.stage_M(l, st, xsrc, xr)
                if upto == "M": return
                self.stage_C(l, st, xr)
            if upto == "C": return
            self.exchange_tails(self.xpredF[:, :, :], self.b_xpredF)
            P.barrier()
            for st in range(nst):
                self.stage_F(l, st, xr)
            if upto == "F" and l == 0: return
            xsrc = xr
            P.pool_alt = None
            self.store_q = "pool"
        self.final_norm(xr)


_WNAMES = ["w_in", "w_fox_o", "w_conv_o", "w_ret_o", "w_out", "w_cq", "w_ckv", "w_co", "w_up", "w_down"]


def _core_inputs(cfg, inp, c, sharded=True):
    b, p = c // 2, c % 2
    ct = const_tables(cfg, p)
    xb = np.asarray(inp["x"][b], np.float32)
    d = dict(xT=x_to_blocks(cfg, xb, p), xpred0=xpred_host(cfg, xb, p),
             memT=np.ascontiguousarray(np.asarray(inp["mem"][b], np.float32).T.reshape(cfg.KD, 128, cfg.MEM)
                                       .transpose(1, 0, 2).reshape(128, -1)),
             vecs=build_vecs(cfg, inp, p), rope=ct["rope"], decq=ct["decq"], dmaskT=ct["dmaskT"], dkT=ct["dkT"],
             masks=ct["masks"], pm=ct["pm"], ident=ct["ident"])
    for nm in _WNAMES:
        w = np.asarray(inp[nm], dtype=np.float32)
        if sharded:
            w = w[:, shard_rows(w.shape[1], w.shape[2], c), :]
        d[nm] = np.ascontiguousarray(w)
    return d


def kernel_impl(cfg, inputs, sharded=True):
    inp = {k: np.asarray(v) for k, v in inputs.items()}
    m = MK(cfg)
    m.build(sharded=sharded)
    m.P.emit()
    in_maps = [_core_inputs(cfg, inp, c, sharded) for c in range(8)]
    res = run_bass_kernel_spmd(m.P.nc, in_maps, core_ids=list(range(8)))
    out = np.zeros((cfg.B, cfg.S, cfg.D), np.float32)
    for c in range(8):
        blocks_to_x(cfg, np.asarray(res.results[c]["out"], np.float32).reshape(cfg.NBo, 128, cfg.KD * T), c % 2,
                    out[c // 2])
    return out


def kernel(**inputs):
    return kernel_impl(Cfg(), inputs)
```
